# Optimizing a Trainium2 kernel written in Bass

```python
import jax, jax.numpy as jnp
from jax import lax
import numpy as np

D_MODEL = 1024
BATCH = 8
SEQ = 4096
DEPTH = 1

GRID_W = 64
CTX_LEN = 256
D_MIX = D_MODEL
C_CONV = D_MIX // 2
CONV_K = 31
N_HEADS = 8
QK_NOPE = 64
QK_ROPE = 32
V_DIM = 64
Q_LORA = 384
KV_LORA = 256
D_FF = 2816
FFN_K = 3
Q_BLOCK = 128
ROPE_THETA = 10000.0
LN_EPS = 1e-5
ALPHA = (2 * DEPTH) ** 0.25
BETA = (8 * DEPTH) ** -0.25
D_IN = 2 * C_CONV + Q_LORA + KV_LORA + QK_ROPE
SOFTMAX_SCALE = (QK_NOPE + QK_ROPE) ** -0.5

kernel_name = 'hybrid_conformer_mla_convffn_prefix_ctx'


def layer_norm(x, g=None, b=None):
    xf = x.astype(jnp.float32)
    mu = xf.mean(-1, keepdims=True)
    var = jnp.square(xf - mu).mean(-1, keepdims=True)
    y = (xf - mu) * lax.rsqrt(var + LN_EPS)
    if g is not None:
        y = y * g.astype(jnp.float32) + b.astype(jnp.float32)
    return y.astype(x.dtype)


def rms_norm(x, g):
    xf = x.astype(jnp.float32)
    y = xf * lax.rsqrt(jnp.square(xf).mean(-1, keepdims=True) + LN_EPS) * g.astype(jnp.float32)
    return y.astype(x.dtype)


def dwconv(x, w, b):
    k = w.shape[0]
    y = lax.conv_general_dilated(x, w[:, None, :], window_strides=(1,),
                                 padding=[(k // 2, k // 2)],
                                 dimension_numbers=('NWC', 'WIO', 'NWC'),
                                 feature_group_count=x.shape[-1])
    return y + b


def axial_rope_tables(n, dtype):
    rows = n // GRID_W
    row = jnp.repeat(jnp.arange(rows), GRID_W)
    col = jnp.tile(jnp.arange(GRID_W), rows)
    n_freq = QK_ROPE // 4
    inv = ROPE_THETA ** (-jnp.arange(n_freq, dtype=jnp.float32) / n_freq)
    ang = jnp.concatenate([row[:, None] * inv, col[:, None] * inv], axis=-1)
    return jnp.cos(ang).astype(dtype), jnp.sin(ang).astype(dtype)


def apply_rope(x, cos, sin):
    half = x.shape[-1] // 2
    x1, x2 = x[..., :half], x[..., half:]
    return jnp.concatenate([x1 * cos - x2 * sin, x2 * cos + x1 * sin], axis=-1)


def mixer_inputs(h, w_in):
    z = h @ w_in
    return jnp.split(z, [2 * C_CONV, 2 * C_CONV + Q_LORA, 2 * C_CONV + Q_LORA + KV_LORA], axis=-1)


def conformer_conv(u, dw_w, dw_b, ln_g, ln_b):
    a, gte = jnp.split(u, 2, axis=-1)
    v = a * jax.nn.sigmoid(gte)
    v = dwconv(v, dw_w, dw_b)
    return jax.nn.silu(layer_norm(v, ln_g, ln_b))


def mla_queries(cq, q_norm_g, w_uq, cos, sin):
    b, n = cq.shape[:2]
    q = (rms_norm(cq, q_norm_g) @ w_uq).reshape(b, n, N_HEADS, QK_NOPE + QK_ROPE)
    q_nope, q_rope = q[..., :QK_NOPE], q[..., QK_NOPE:]
    if cos is not None:
        q_rope = apply_rope(q_rope, cos[:, None, :], sin[:, None, :])
    return q_nope, q_rope


def mla_keys_values(ckv, kr, kv_norm_g, w_ukv, cos, sin):
    b, n = ckv.shape[:2]
    kv = (rms_norm(ckv, kv_norm_g) @ w_ukv).reshape(b, n, N_HEADS, QK_NOPE + V_DIM)
    k_nope, v = kv[..., :QK_NOPE], kv[..., QK_NOPE:]
    k_rope = kr if cos is None else apply_rope(kr, cos, sin)
    return k_nope, k_rope, v


def attend(q_nope, q_rope, k_nope, k_rope, v):
    s = (jnp.einsum('bqhd,bkhd->bhqk', q_nope, k_nope)
         + jnp.einsum('bqhr,bkr->bhqk', q_rope, k_rope)).astype(jnp.float32) * SOFTMAX_SCALE
    p = jax.nn.softmax(s, axis=-1).astype(v.dtype)
    return jnp.einsum('bhqk,bkhd->bqhd', p, v)


def attend_blocked(q_nope, q_rope, k_nope, k_rope, v):
    b, n = q_nope.shape[:2]
    nb = n // Q_BLOCK
    qn = q_nope.reshape(b, nb, Q_BLOCK, N_HEADS, QK_NOPE).transpose(1, 0, 2, 3, 4)
    qr = q_rope.reshape(b, nb, Q_BLOCK, N_HEADS, QK_ROPE).transpose(1, 0, 2, 3, 4)
    out = lax.map(lambda qs: attend(qs[0], qs[1], k_nope, k_rope, v), (qn, qr))
    return out.transpose(1, 0, 2, 3, 4).reshape(b, n, N_HEADS, V_DIM)


def merge_groups(conv_out, attn_out, w_o, b_o):
    b, n = conv_out.shape[:2]
    cat = jnp.concatenate([conv_out, attn_out.reshape(b, n, N_HEADS * V_DIM)], axis=-1)
    return cat @ w_o + b_o


def conv_ffn(h, w_up, dw_w, dw_b, w_down, b_down):
    u = dwconv(h @ w_up, dw_w, dw_b)
    g, val = jnp.split(u, 2, axis=-1)
    return (jax.nn.silu(g) * val) @ w_down + b_down


def setup_inputs(seed: int = 0) -> dict:
    key = jax.random.key(seed)
    ks = jax.random.split(key, 32)
    L = DEPTH

    def nrm(k, shape, s):
        return s * jax.random.normal(k, shape, jnp.float32)

    return {
        'x': nrm(ks[0], (BATCH, SEQ, D_MODEL), 1.0),
        'c': nrm(ks[1], (BATCH, D_MODEL), 1.0),
        'ctx': nrm(ks[2], (BATCH, CTX_LEN, D_MODEL), 1.0),
        'c_ctx': nrm(ks[3], (D_MODEL,), 1.0),
        'w_ada': nrm(ks[4], (L, D_MODEL, 6 * D_MODEL), D_MODEL ** -0.5),
        'b_ada': nrm(ks[5], (L, 6 * D_MODEL), 0.02),
        'w_in': nrm(ks[6], (L, D_MODEL, D_IN), D_MODEL ** -0.5),
        'conv_dw_w': nrm(ks[7], (L, CONV_K, C_CONV), CONV_K ** -0.5),
        'conv_dw_b': nrm(ks[8], (L, C_CONV), 0.02),
        'conv_ln_g': 1.0 + nrm(ks[9], (L, C_CONV), 0.05),
        'conv_ln_b': nrm(ks[10], (L, C_CONV), 0.02),
        'q_norm_g': 1.0 + nrm(ks[11], (L, Q_LORA), 0.05),
        'w_uq': nrm(ks[12], (L, Q_LORA, N_HEADS * (QK_NOPE + QK_ROPE)), Q_LORA ** -0.5),
        'kv_norm_g': 1.0 + nrm(ks[13], (L, KV_LORA), 0.05),
        'w_ukv': nrm(ks[14], (L, KV_LORA, N_HEADS * (QK_NOPE + V_DIM)), KV_LORA ** -0.5),
        'w_o': nrm(ks[15], (L, D_MIX, D_MODEL), BETA * D_MIX ** -0.5),
        'b_o': nrm(ks[16], (L, D_MODEL), 0.02),
        'ln1_g': 1.0 + nrm(ks[17], (L, D_MODEL), 0.05),
        'ln1_b': nrm(ks[18], (L, D_MODEL), 0.02),
        'w_up': nrm(ks[19], (L, D_MODEL, 2 * D_FF), D_MODEL ** -0.5),
        'ffn_dw_w': nrm(ks[20], (L, FFN_K, 2 * D_FF), FFN_K ** -0.5),
        'ffn_dw_b': nrm(ks[21], (L, 2 * D_FF), 0.02),
        'w_down': nrm(ks[22], (L, D_FF, D_MODEL), BETA * D_FF ** -0.5),
        'b_down': nrm(ks[23], (L, D_MODEL), 0.02),
        'ln2_g': 1.0 + nrm(ks[24], (L, D_MODEL), 0.05),
        'ln2_b': nrm(ks[25], (L, D_MODEL), 0.02),
    }


def reference(x, c, ctx, c_ctx, w_ada, b_ada, w_in, conv_dw_w, conv_dw_b, conv_ln_g, conv_ln_b,
              q_norm_g, w_uq, kv_norm_g, w_ukv, w_o, b_o, ln1_g, ln1_b,
              w_up, ffn_dw_w, ffn_dw_b, w_down, b_down, ln2_g, ln2_b):
    n = x.shape[1]
    cos, sin = axial_rope_tables(n, x.dtype)
    x = layer_norm(x)
    ctx = layer_norm(ctx)
    for i in range(DEPTH):
        last = i == DEPTH - 1
        mod = jax.nn.silu(c) @ w_ada[i] + b_ada[i]
        sh1, sc1, g1, sh2, sc2, g2 = jnp.split(mod[:, None, :], 6, axis=-1)
        modc = jax.nn.silu(c_ctx) @ w_ada[i] + b_ada[i]
        sh1c, sc1c, g1c, sh2c, sc2c, g2c = jnp.split(modc, 6)

        u_c, cq_c, ckv_c, kr_c = mixer_inputs(ctx * (1 + sc1c) + sh1c, w_in[i])
        k_nope_c, k_rope_c, v_c = mla_keys_values(ckv_c, kr_c, kv_norm_g[i], w_ukv[i], None, None)

        u_l, cq_l, ckv_l, kr_l = mixer_inputs(x * (1 + sc1) + sh1, w_in[i])
        k_nope_l, k_rope_l, v_l = mla_keys_values(ckv_l, kr_l, kv_norm_g[i], w_ukv[i], cos, sin)
        q_nope_l, q_rope_l = mla_queries(cq_l, q_norm_g[i], w_uq[i], cos, sin)
        attn_l = attend_blocked(q_nope_l, q_rope_l,
                                jnp.concatenate([k_nope_c, k_nope_l], axis=1),
                                jnp.concatenate([k_rope_c, k_rope_l], axis=1),
                                jnp.concatenate([v_c, v_l], axis=1))
        conv_l = conformer_conv(u_l, conv_dw_w[i], conv_dw_b[i], conv_ln_g[i], conv_ln_b[i])
        y_l = merge_groups(conv_l, attn_l, w_o[i], b_o[i])
        x_mid = layer_norm(ALPHA * x + g1 * y_l, ln1_g[i], ln1_b[i])

        if not last:
            q_nope_c, q_rope_c = mla_queries(cq_c, q_norm_g[i], w_uq[i], None, None)
            attn_c = attend(q_nope_c, q_rope_c, k_nope_c, k_rope_c, v_c)
            conv_c = conformer_conv(u_c, conv_dw_w[i], conv_dw_b[i], conv_ln_g[i], conv_ln_b[i])
            y_c = merge_groups(conv_c, attn_c, w_o[i], b_o[i])
            ctx = layer_norm(ALPHA * ctx + g1c * y_c, ln1_g[i], ln1_b[i])
            f_c = conv_ffn(ctx * (1 + sc2c) + sh2c, w_up[i], ffn_dw_w[i], ffn_dw_b[i], w_down[i], b_down[i])
            ctx = layer_norm(ALPHA * ctx + g2c * f_c, ln2_g[i], ln2_b[i])

        f_l = conv_ffn(x_mid * (1 + sc2) + sh2, w_up[i], ffn_dw_w[i], ffn_dw_b[i], w_down[i], b_down[i])
        x = layer_norm(ALPHA * x_mid + g2 * f_l, ln2_g[i], ln2_b[i])
    return x
```

```python
import numpy as np
import ml_dtypes
from contextlib import ExitStack
import concourse.bass as bass
import concourse.mybir as mybir
from concourse.bass_utils import run_bass_kernel_spmd

F32 = mybir.dt.float32
BF16 = mybir.dt.bfloat16
U8 = mybir.dt.uint8
AF = mybir.ActivationFunctionType
ALU = mybir.AluOpType

D = 1024
KD = 8
DIN = 1696
DINX = 1728
NH = 8
DFF = 2816
MUP = 44
MFF = 22
ALPHA = 2.0 ** 0.25
EPS = 1e-5
GRID_W = 64
SCALE = 96.0 ** -0.5

PV = {}
_o = 0
for _n, _c in [('cs', 16), ('b_ada', 48), ('conv_b', 4), ('conv_g', 4), ('conv_lb', 4), ('conv_w', 124),
               ('qg', 3), ('kvg', 2), ('b_o', 8), ('ln1_g', 8), ('ln1_b', 8), ('ffn_w', 132), ('ffn_b', 44),
               ('b_down', 8), ('ln2_g', 8), ('ln2_b', 8)]:
    PV[_n] = (_o, _c)
    _o += _c
NV = _o


def colT(v, nchunk):
    return np.ascontiguousarray(np.asarray(v, np.float32).reshape(nchunk, 128).T)


def pack_pvec(inp, b):
    pv = np.zeros((128, NV), np.float32)

    def put(name, arr):
        o, c = PV[name]
        assert arr.shape == (128, c), (name, arr.shape, c)
        pv[:, o:o + c] = arr

    cs = np.zeros((128, 16), np.float32)
    cs[:, 0::2] = colT(inp['c'][b], 8)
    cs[:, 1::2] = colT(inp['c_ctx'], 8)
    put('cs', cs)
    put('b_ada', colT(inp['b_ada'][0], 48))
    put('conv_b', colT(inp['conv_dw_b'][0], 4))
    put('conv_g', colT(inp['conv_ln_g'][0], 4))
    put('conv_lb', colT(inp['conv_ln_b'][0], 4))
    cw = np.asarray(inp['conv_dw_w'][0], np.float32)
    put('conv_w', np.ascontiguousarray(cw.reshape(31, 4, 128).transpose(2, 1, 0).reshape(128, 124)))
    put('qg', colT(inp['q_norm_g'][0], 3))
    put('kvg', colT(inp['kv_norm_g'][0], 2))
    put('b_o', colT(inp['b_o'][0], 8))
    put('ln1_g', colT(inp['ln1_g'][0], 8))
    put('ln1_b', colT(inp['ln1_b'][0], 8))
    fw = np.asarray(inp['ffn_dw_w'][0], np.float32)
    put('ffn_w', np.ascontiguousarray(fw.reshape(3, MUP, 128).transpose(2, 1, 0).reshape(128, 132)))
    put('ffn_b', colT(inp['ffn_dw_b'][0], MUP))
    put('b_down', colT(inp['b_down'][0], 8))
    put('ln2_g', colT(inp['ln2_g'][0], 8))
    put('ln2_b', colT(inp['ln2_b'][0], 8))
    return pv


def rope_tables(n):
    rows = n // GRID_W
    row = np.repeat(np.arange(rows), GRID_W).astype(np.float32)
    col = np.tile(np.arange(GRID_W), rows).astype(np.float32)
    n_freq = 8
    inv = (np.float32(10000.0) ** (-np.arange(n_freq, dtype=np.float32) / np.float32(n_freq))).astype(np.float32)
    ang = np.concatenate([row[:, None] * inv, col[:, None] * inv], axis=-1).astype(np.float32)
    cos = np.cos(ang).astype(np.float32).T
    sin = np.sin(ang).astype(np.float32).T
    cosT = np.ascontiguousarray(np.tile(cos, (6, 1)))
    sinT = np.ascontiguousarray(np.tile(sin, (6, 1)))
    return cosT, sinT


class Buf:
    __slots__ = ('wr', 'rd', 'prev')

    def __init__(self):
        self.wr = {}
        self.rd = {}
        self.prev = {}

    def begin(self):
        self.prev = _merge(_merge({}, self.wr), self.rd)
        self.wr = {}
        self.rd = {}


def _merge(d, t):
    if t is None:
        return d
    if isinstance(t, tuple) and len(t) == 2 and isinstance(t[0], str):
        k, v = t
        if d.get(k, 0) < v:
            d[k] = v
    elif isinstance(t, dict):
        for k, v in t.items():
            if d.get(k, 0) < v:
                d[k] = v
    else:
        for x in t:
            _merge(d, x)
    return d


class Prog:
    ENGS = ('pe', 'act', 'dve', 'pool', 'sp')
    DMA_LIMIT = 8

    def __init__(self, nc, es):
        self.nc = nc
        self.es = es
        self.q = {k: [] for k in self.ENGS}
        self.sem = {}
        self.cnt = {}
        self.waited = {k: {} for k in self.ENGS}
        self.dma_fifo = {}
        for k in self.ENGS:
            self.newsem(k)
        self.nsem = 0

    def newsem(self, key):
        if key not in self.sem:
            self.sem[key] = self.es.enter_context(self.nc.semaphore("s_" + str(key)))
            self.cnt[key] = 0
        return key

    def _waits(self, eng, waits):
        d = _merge({}, waits)
        for key, v in d.items():
            if self.waited[eng].get(key, 0) < v:
                self.waited[eng][key] = v
                s = self.sem[key]
                self.q[eng].append(lambda e, s=s, v=v: e.wait_ge(s, v))

    def op(self, eng, fn, waits=(), inc=True):
        self._waits(eng, waits)
        if inc:
            self.cnt[eng] += 1
            s = self.sem[eng]
            self.q[eng].append(lambda e, fn=fn, s=s: fn(e).then_inc(s, 1))
            return (eng, self.cnt[eng])
        self.q[eng].append(lambda e, fn=fn: fn(e))
        return None

    def do(self, eng, fn, reads=(), writes=(), pwrites=(), extra=()):
        w = _merge({}, extra)
        for b in reads:
            _merge(w, b.wr)
        for b in writes:
            b.begin()
            _merge(w, b.prev)
        for b in pwrites:
            _merge(w, b.prev)
        tok = self.op(eng, fn, w, inc=True)
        for b in reads:
            _merge(b.rd, tok)
        for b in writes:
            _merge(b.wr, tok)
        for b in pwrites:
            _merge(b.wr, tok)
        return tok

    def dma(self, eng, semkey, out, in_, reads=(), writes=(), pwrites=(), extra=(), **kw):
        self.newsem(semkey)
        w = _merge({}, extra)
        for b in reads:
            _merge(w, b.wr)
        for b in writes:
            b.begin()
            _merge(w, b.prev)
        for b in pwrites:
            _merge(w, b.prev)
        self._waits(eng, w)
        fifo = self.dma_fifo.setdefault(eng, [])
        while len(fifo) >= self.DMA_LIMIT:
            k_old = fifo.pop(0)
            self._waits(eng, (k_old, self.cnt[k_old]))
        fifo.append(semkey)
        self.cnt[semkey] += 16
        s = self.sem[semkey]
        self.q[eng].append(lambda e, s=s, out=out, in_=in_, kw=kw: e.dma_start(out=out, in_=in_, **kw).then_inc(s, 16))
        tok = (semkey, self.cnt[semkey])
        for b in reads:
            _merge(b.rd, tok)
        for b in writes:
            _merge(b.wr, tok)
        for b in pwrites:
            _merge(b.wr, tok)
        return tok

    def barrier(self, extra=()):
        toks = _merge({}, extra)
        for k in self.ENGS:
            if k == 'sp':
                continue
            if self.cnt[k] > 0:
                _merge(toks, (k, self.cnt[k]))
        for k, v in self.cnt.items():
            if k not in self.ENGS and v > 0:
                _merge(toks, (k, v))
        for k in self.ENGS:
            self._waits(k, toks)
        return toks

    def run(self):
        with self.nc.Block() as block:
            @block.tensor
            def _(e):
                for f in self.q['pe']:
                    f(e)

            @block.scalar
            def _(e):
                for f in self.q['act']:
                    f(e)

            @block.vector
            def _(e):
                for f in self.q['dve']:
                    f(e)

            @block.gpsimd
            def _(e):
                for f in self.q['pool']:
                    f(e)

            @block.sync
            def _(e):
                for f in self.q['sp']:
                    f(e)


class Arena:
    def __init__(self, ap, size):
        self.ap = ap
        self.size = size
        self.top = 0
        self.marks = []

    def alloc(self, nbytes, dt, shape=None):
        off = self.top
        self.top += (nbytes + 31) // 32 * 32
        assert self.top <= self.size, ("arena overflow", self.top, self.size)
        v = self.ap[:, off:off + nbytes].bitcast(dt)
        return v

    def raw(self, nbytes):
        off = self.top
        self.top += (nbytes + 31) // 32 * 32
        assert self.top <= self.size, ("arena overflow", self.top, self.size)
        return self.ap[:, off:off + nbytes]

    def mark(self):
        self.marks.append(self.top)

    def release(self):
        self.top = self.marks.pop()


def build_program(N=4096, NC=256, upto='all', dbg=()):
    assert N % 512 == 0 and NC % 256 == 0
    PADC = N + 32
    NK = N + NC
    NKT = NK // 128
    NCH = N // 512
    nc = bass.Bass("TRN2", target_bir_lowering=False)
    dr = {}

    def din(name, shape, dt=F32):
        dr[name] = nc.dram_tensor(name, shape, dt, kind="ExternalInput").ap()
        return dr[name]

    if upto != 'S':
        x_d = din("x", [N, D])
        ctx_d = din("ctx", [NC, D])
        cos_d = din("cosT", [96, N])
        sin_d = din("sinT", [96, N])
    pvec_d = din("pvec", [128, NV])
    ident_d = din("ident", [128, 128])
    wada_d = din("w_ada", [D, 6 * D])
    win_d = din("w_in", [D, DIN])
    if upto not in ('S', 'A', 'B'):
        wuq_d = din("w_uq", [384, 768])
        wukv_d = din("w_ukv", [256, 1024])
    if upto not in ('S', 'A', 'B', 'C1'):
        wo_d = din("w_o", [D, D])
        wup_d = din("w_up", [D, 2 * DFF])
        wdn_d = din("w_down", [DFF, D])
    out_d = nc.dram_tensor("out", [N, D], F32, kind="ExternalOutput").ap()
    xts_d = nc.dram_tensor("xts", [8, 128, N], F32).ap()
    wups_d = nc.dram_tensor("wups", [11, 128, 8 * 512], BF16).ap()
    wdns_d = nc.dram_tensor("wdns", [128, MFF * 1024], BF16).ap()
    dbg_d = {}
    for name, shape in dbg:
        dbg_d[name] = nc.dram_tensor(name, shape, F32, kind="ExternalOutput").ap()

    with ExitStack() as es:
        p = Prog(nc, es)
        ARENA_BYTES = 206 * 1024
        arena_t = es.enter_context(nc.sbuf_tensor("arena", [128, ARENA_BYTES], U8))
        ps_t = es.enter_context(nc.psum_tensor("ps", [128, 8, 512], F32))
        A = Arena(arena_t, ARENA_BYTES)

        def bank(i, n=1):
            if n == 1:
                return ps_t[:, i, :]
            return ps_t[:, i:i + n, :].rearrange("p a b -> p (a b)")

        pvec = A.alloc(NV * 4, F32)
        dcol = A.alloc(160 * 4, F32)
        identF = A.alloc(512, F32)
        identB = A.alloc(256, BF16)
        onesB = A.alloc(256, BF16)
        onesF = A.alloc(512, F32)
        kcol = A.alloc(32, F32)
        stats = A.alloc(40 * 2 * 4, F32).rearrange("p (t s) -> p t s", s=2)
        off_dead0 = A.top
        cqn = A.alloc(3 * N * 2, BF16).rearrange("p (m t) -> p m t", m=3)
        ckvn = A.alloc(2 * NK * 2, BF16).rearrange("p (m t) -> p m t", m=2)
        KT = A.alloc(2 * NK * 2, BF16).rearrange("p (s t) -> p s t", s=2)
        off_dead1 = A.top
        cat = A.alloc(8 * PADC * 2, BF16).rearrange("p (c t) -> p c t", c=8)
        b_pvec, b_dcol, b_const, b_stats = Buf(), Buf(), Buf(), Buf()
        b_cqn = [Buf() for _ in range(NCH)]
        b_ckvn = [Buf() for _ in range(NCH + 1)]
        b_krope = [Buf() for _ in range(NCH + 1)]

        def pv(name, j=None, n=1):
            o, c = PV[name]
            if j is None:
                return pvec[:, o:o + c]
            return pvec[:, o + j:o + j + n]

        DC = {}
        _off = [0]

        def dc_alloc(name, n):
            DC[name] = (_off[0], n)
            _off[0] += n
            assert _off[0] <= 160

        for nm, n in [('modL', 48), ('modC', 48), ('sc1p', 8), ('sc1pc', 8), ('g1bo', 8), ('sc2p', 8), ('G2', 8),
                      ('B2', 8), ('A3', 8), ('B3', 8)]:
            dc_alloc(nm, n)

        def dcv(name, j=None, n=1):
            o, c = DC[name]
            if j is None:
                return dcol[:, o:o + c]
            return dcol[:, o + j:o + j + n]

        def modL(s, j=None):
            o = DC['modL'][0] + s * 8
            return dcol[:, o:o + 8] if j is None else dcol[:, o + j:o + j + 1]

        def modC(s, j=None):
            o = DC['modC'][0] + s * 8
            return dcol[:, o:o + 8] if j is None else dcol[:, o + j:o + j + 1]

        p.dma('sp', 'ld0', pvec, pvec_d, writes=[b_pvec])
        p.dma('sp', 'ld1', identF, ident_d, writes=[b_const])
        p.do('pool', lambda e: e.memset(dcol, 0.0), writes=[b_dcol])
        p.do('pool', lambda e: e.memset(stats.rearrange("p t s -> p (t s)"), 0.0), writes=[b_stats])
        t1 = p.do('pool', lambda e: e.tensor_copy(out=identB, in_=identF), reads=[b_const])
        t2 = p.do('pool', lambda e: e.memset(onesB, 1.0))
        t3 = p.do('pool', lambda e: e.memset(onesF, 1.0))
        t4 = p.do('pool', lambda e: e.memset(kcol[:, 0:1], EPS))
        t5 = p.do('pool', lambda e: e.memset(kcol[:, 1:2], 0.0))
        tconst = _merge({}, [t1, t2, t3, t4, t5])
        eps_c = kcol[:, 0:1]
        import os as _os
        _stop = _os.environ.get('STOP', '')

        class _Early(Exception):
            pass

        def early(tag):
            if _stop == tag:
                p.barrier()
                p.dma('sp', 'dbg2', dbg_d['d_dcol'], dcol)
                p._waits('sp', ('dbg2', p.cnt['dbg2']))
                p.run()
                raise _Early(nc)

        A.mark()
        w_in_bf = A.alloc(8 * DINX * 2, BF16).rearrange("p (k c) -> p k c", k=8)
        b_win = Buf()
        A.mark()
        wada = [A.alloc(6 * D * 4, F32) for _ in range(2)]
        b_wada = [Buf(), Buf()]
        cs2 = A.alloc(16 * 4, F32)
        b_cs2 = Buf()
        early('s0')
        p.do('act', lambda e: e.activation(out=cs2, in_=pv('cs'), func=AF.Silu), reads=[b_pvec], writes=[b_cs2])
        early('s1')
        wqs = [wada[0][:, 0:3072], wada[0][:, 3072:6144], wada[1][:, 0:3072], wada[1][:, 3072:6144]]
        b_wqs = [Buf() for _ in range(4)]
        rowt = [A.alloc(2048, F32) for _ in range(2)]
        b_rowt = [Buf(), Buf()]
        accb = [Buf() for _ in range(6)]
        b_T = Buf()
        b_T.begin()
        mpsT = bank(6)[:, 0:96]
        qi = 0
        win_k = [0]
        b_win.begin()
        win_st = wqs[3][:, 0:DIN]

        def win_piece():
            k = win_k[0]
            if k >= 8:
                return
            win_k[0] += 1
            p.dma('sp', 'wq3', win_st, win_d[k * 128:(k + 1) * 128, :], writes=[b_wqs[3]])
            p.do('dve', lambda e: e.tensor_copy(out=w_in_bf[:, k, 0:DIN], in_=win_st), reads=[b_wqs[3]], pwrites=[b_win])
            p.do('dve', lambda e: e.tensor_scalar(out=w_in_bf[:, k, DIN:DIN + 16], in0=win_st[:, 1680:1696], scalar1=-1.0, scalar2=None, op0=ALU.mult),
                 reads=[b_wqs[3]], pwrites=[b_win])
            p.do('dve', lambda e: e.tensor_copy(out=w_in_bf[:, k, DIN + 16:DIN + 32], in_=win_st[:, 1664:1680]), reads=[b_wqs[3]], pwrites=[b_win])

        for r in range(2):
            for k in range(8):
                sl = qi % 3
                qi += 1
                p.dma('sp', 'wq%d' % sl, wqs[sl], wada_d[k * 128:(k + 1) * 128, r * 3072:(r + 1) * 3072], writes=[b_wqs[sl]])
                if qi % 2 == 0:
                    win_piece()

                def mm(e, sl=sl, k=k):
                    ins = None
                    for g in range(6):
                        ins = e.matmul(bank(g)[0:2, :], lhsT=cs2[:, 2 * k:2 * k + 2], rhs=wqs[sl][:, g * 512:(g + 1) * 512], start=(k == 0), stop=(k == 7))
                    return ins
                if k == 0:
                    p.do('pe', mm, reads=[b_wqs[sl], b_cs2], writes=accb)
                else:
                    p.do('pe', mm, reads=[b_wqs[sl], b_cs2], pwrites=accb)
            for g in range(6):
                ri = (r * 6 + g) % 2
                if g % 2 == 0:
                    p.do('act', lambda e, ri=ri, g=g: e.activation(out=rowt[ri][0:2, :], in_=bank(g)[0:2, :], func=AF.Copy), reads=[accb[g]], writes=[b_rowt[ri]])
                else:
                    p.do('dve', lambda e, ri=ri, g=g: e.tensor_copy(out=rowt[ri][0:2, :], in_=bank(g)[0:2, :]), reads=[accb[g]], writes=[b_rowt[ri]])

                def tr(e, ri=ri, r=r, g=g):
                    ins = None
                    for q_ in range(4):
                        f = r * 24 + g * 4 + q_
                        ins = e.matmul(mpsT[:, 2 * f:2 * f + 2], lhsT=rowt[ri][0:2, q_ * 128:(q_ + 1) * 128], rhs=identF[0:2, 0:2], start=True, stop=True)
                    return ins
                p.do('pe', tr, reads=[b_rowt[ri]], pwrites=[b_T], extra=tconst)
        early('s2')
        acc3 = mpsT.rearrange("p (f t) -> p f t", t=2)
        p.do('dve', lambda e: e.tensor_tensor(out=dcv('modL'), in0=acc3[:, :, 0], in1=pv('b_ada'), op=ALU.add),
             reads=[b_T, b_pvec], writes=[b_dcol])
        p.do('dve', lambda e: e.tensor_tensor(out=dcv('modC'), in0=acc3[:, :, 1], in1=pv('b_ada'), op=ALU.add),
             reads=[b_T, b_pvec, b_dcol], writes=[b_dcol])
        early('s3')

        def dve_small(fn):
            p.do('dve', fn, reads=[b_dcol, b_pvec], writes=[b_dcol])

        dve_small(lambda e: e.tensor_scalar(out=dcv('sc1p'), in0=modL(1), scalar1=1.0, scalar2=None, op0=ALU.add))
        dve_small(lambda e: e.tensor_scalar(out=dcv('sc1pc'), in0=modC(1), scalar1=1.0, scalar2=None, op0=ALU.add))
        dve_small(lambda e: e.tensor_tensor(out=dcv('g1bo'), in0=modL(2), in1=pv('b_o'), op=ALU.mult))
        dve_small(lambda e: e.tensor_scalar(out=dcv('sc2p'), in0=modL(4), scalar1=1.0, scalar2=None, op0=ALU.add))
        dve_small(lambda e: e.tensor_tensor(out=dcv('G2'), in0=pv('ln1_g'), in1=dcv('sc2p'), op=ALU.mult))
        dve_small(lambda e: e.tensor_tensor(out=dcv('B2'), in0=pv('ln1_b'), in1=dcv('sc2p'), op=ALU.mult))
        dve_small(lambda e: e.tensor_tensor(out=dcv('B2'), in0=dcv('B2'), in1=modL(3), op=ALU.add))
        dve_small(lambda e: e.tensor_scalar(out=dcv('A3'), in0=pv('ln1_g'), scalar1=ALPHA, scalar2=None, op0=ALU.mult))
        dve_small(lambda e: e.tensor_tensor(out=dcv('B3'), in0=modL(5), in1=pv('b_down'), op=ALU.mult))
        dve_small(lambda e: e.scalar_tensor_tensor(out=dcv('B3'), in0=pv('ln1_b'), scalar=ALPHA, in1=dcv('B3'),
                                                  op0=ALU.mult, op1=ALU.add))

        early('s4')
        while win_k[0] < 8:
            win_piece()
        p.barrier()
        A.release()
        if upto == 'S':
            p.dma('sp', 'dbg2', dbg_d['d_dcol'], dcol)
            p._waits('sp', ('dbg2', p.cnt['dbg2']))
            p.run()
            return nc

        NXT = 4
        xt = [A.alloc(D * 4, F32) for _ in range(NXT)]
        b_xt = [Buf() for _ in range(NXT)]
        cqf = [A.alloc(512 * 4, F32) for _ in range(5)]
        b_cqf = [Buf() for _ in range(5)]
        sqb = [A.alloc(512 * 2, BF16) for _ in range(5)]
        b_sqb = [Buf() for _ in range(5)]
        tmpf = [A.alloc(512 * 4, F32) for _ in range(6)]
        b_tmpf = [Buf() for _ in range(6)]
        cst = [A.alloc(2 * 512 * 4, F32).rearrange("p (s t) -> p s t", s=2) for _ in range(2)]
        b_cst = [Buf(), Buf()]
        st6 = A.alloc(16 * 4 * 4, F32).rearrange("p (s c) -> p s c", s=4)
        b_st6 = [Buf() for _ in range(4)]
        if 4 * PADC >= 12288:
            catflat = cat[:, 0:4, :].rearrange("p c t -> p (c t)")
        else:
            catflat = A.alloc(12288 * 2, BF16)
        hT = [catflat[:, s * 4096:(s + 1) * 4096].rearrange("p (k t) -> p k t", k=8) for s in range(2)]
        xnb = [catflat[:, 8192 + s * 1024: 8192 + (s + 1) * 1024] for s in range(4)]
        b_hT = [[Buf() for _ in range(8)] for _ in range(2)]
        b_xnb = [Buf() for _ in range(4)]
        vT = cat[:, 4:8, :]
        b_vT = [Buf() for _ in range(NCH)]
        b_vpad = Buf()
        p.do('pool', lambda e: e.memset(vT[:, :, 0:15], 0.0), writes=[b_vpad])
        p.do('pool', lambda e: e.memset(vT[:, :, 15 + N:30 + N], 0.0), reads=[b_vpad], pwrites=[b_vpad])
        early('p0')
        tp_ps = ps_t[:, 0:2, :].rearrange("p a b -> p (a b)").bitcast(BF16).rearrange("p (j t) -> p j t", j=8)
        b_tp = Buf()
        zb = [Buf() for _ in range(6)]
        zring = [0]

        def znext():
            i = zring[0] % 6
            zring[0] += 1
            return bank(2 + i), zb[i]

        chunks = [('lat', c * 512, 512, c) for c in range(NCH)] + [('ctx', 0, NC, NCH)]
        xcount = [0]
        tilecount = [0]

        xslots = {}

        def xload(ci):
            kind, t0, nt, cidx = chunks[ci]
            src = x_d if kind == 'lat' else ctx_d
            xslots[ci] = []
            for i in range(nt // 128):
                xs = xcount[0] % NXT
                xcount[0] += 1
                xslots[ci].append(xs)
                p.dma('sp', 'xt%d' % xs, xt[xs], src[t0 + i * 128:t0 + (i + 1) * 128, :], writes=[b_xt[xs]])

        def pro_stats(ci, i):
            xs = xslots[ci][i]
            p.do('dve', lambda e: e.bn_stats(out=st6[:, i, 0:6], in_=xt[xs][:, 0:512]), reads=[b_xt[xs]], writes=[b_st6[i]])
            p.do('dve', lambda e: e.bn_stats(out=st6[:, i, 6:12], in_=xt[xs][:, 512:1024]), reads=[b_xt[xs]], pwrites=[b_st6[i]])
            p.do('dve', lambda e: e.bn_aggr(out=st6[:, i, 12:14], in_=st6[:, i, 0:12]), reads=[b_st6[i]], pwrites=[b_st6[i]])

        def pro_norm(ci):
            kind, t0, nt, cidx = chunks[ci]
            ntile = nt // 128
            gt0 = tile_base[ci]
            bs = [b_st6[i] for i in range(ntile)]
            p.do('act', lambda e: e.activation(out=st6[:, 0:ntile, 14:15], in_=st6[:, 0:ntile, 13:14], func=AF.Sqrt, bias=eps_c, scale=1.0),
                 reads=bs, pwrites=bs, extra=tconst)
            p.do('dve', lambda e: e.reciprocal(out=stats[:, gt0:gt0 + ntile, 0:1], in_=st6[:, 0:ntile, 14:15]), reads=bs, pwrites=[b_stats])
            p.do('dve', lambda e: e.scalar_tensor_tensor(out=stats[:, gt0:gt0 + ntile, 1:2], in0=st6[:, 0:ntile, 12:13], scalar=-1.0,
                                                         in1=stats[:, gt0:gt0 + ntile, 0:1], op0=ALU.mult, op1=ALU.mult),
                 reads=bs + [b_stats], pwrites=[b_stats])
            for i in range(ntile):
                xs = xslots[ci][i]
                gt = gt0 + i
                p.do('act', lambda e, i=i, xs=xs, gt=gt: e.activation(out=xnb[i], in_=xt[xs], func=AF.Identity, scale=stats[:, gt, 0:1], bias=stats[:, gt, 1:2]),
                     reads=[b_xt[xs], b_stats], writes=[b_xnb[i]])

        def pro_transposes(ci):
            kind, t0, nt, cidx = chunks[ci]
            sl = ci % 2
            scol = 'sc1p' if kind == 'lat' else 'sc1pc'
            for b_ in b_hT[sl]:
                b_.begin()
            for i in range(nt // 128):
                def tr(e, i=i):
                    ins = None
                    for j in range(8):
                        ins = e.transpose(out=tp_ps[:, j, (i % 2) * 128:(i % 2) * 128 + 128],
                                          in_=xnb[i][:, j * 128:(j + 1) * 128], identity=identB)
                    return ins
                if i % 2 == 0:
                    p.do('pe', tr, reads=[b_xnb[i]], writes=[b_tp], extra=tconst)
                else:
                    p.do('pe', tr, reads=[b_xnb[i]], pwrites=[b_tp], extra=tconst)
                if i % 2 == 1:
                    pair = i // 2
                    for j in range(8):
                        o = hT[sl][:, j, pair * 256:(pair + 1) * 256]
                        if j < 4:
                            p.do('act', lambda e, o=o, j=j: e.activation(out=o, in_=tp_ps[:, j, :], func=AF.Identity,
                                                                         scale=dcv(scol, j), bias=(modL(0, j) if kind == 'lat' else modC(0, j))),
                                 reads=[b_tp, b_dcol], pwrites=[b_hT[sl][j]])
                        else:
                            p.do('dve', lambda e, o=o, j=j: e.tensor_scalar(out=o, in0=tp_ps[:, j, :], scalar1=dcv(scol, j),
                                                                            scalar2=(modL(0, j) if kind == 'lat' else modC(0, j)),
                                                                            op0=ALU.mult, op1=ALU.add),
                                 reads=[b_tp, b_dcol], pwrites=[b_hT[sl][j]])

        def zgroup(sl, nt, c0, m):
            z, zbuf = znext()

            def mm(e, z=z):
                ins = None
                for k in range(8):
                    ins = e.matmul(z[0:m, 0:nt], lhsT=w_in_bf[:, k, c0:c0 + m], rhs=hT[sl][:, k, 0:nt],
                                   start=(k == 0), stop=(k == 7))
                return ins
            p.do('pe', mm, reads=b_hT[sl] + [b_win], writes=[zbuf])
            return z, zbuf

        tmpi = [0]

        def tnext():
            i = tmpi[0] % 6
            tmpi[0] += 1
            return tmpf[i], b_tmpf[i]

        def rms_front(sl, nt, c0, nm, slot0):
            for m in range(nm):
                z, zbuf = zgroup(sl, nt, c0 + m * 128, 128)
                f, bf_ = cqf[slot0 + m], b_cqf[slot0 + m]
                p.do('act', lambda e, z=z, f=f: e.activation(out=f[:, 0:nt], in_=z[:, 0:nt], func=AF.Copy),
                     reads=[zbuf], writes=[bf_])
                p.do('pool', lambda e, f=f, m=m: e.tensor_tensor(out=sqb[slot0 + m][:, 0:nt], in0=f[:, 0:nt], in1=f[:, 0:nt], op=ALU.mult),
                     reads=[bf_], writes=[b_sqb[slot0 + m]])

        def rms_back(nt, nm, slot0, dim, dst, dstbuf, tcol0):
            z, zbuf = znext()

            def mm(e, z=z):
                ins = None
                for m in range(nm):
                    ins = e.matmul(z[:, 0:nt], lhsT=onesB, rhs=sqb[slot0 + m][:, 0:nt], start=(m == 0), stop=(m == nm - 1))
                return ins
            p.do('pe', mm, reads=[b_sqb[slot0 + m] for m in range(nm)], writes=[zbuf], extra=tconst)
            r, rb = tnext()
            p.do('act', lambda e, z=z, r=r: e.activation(out=r[:, 0:nt], in_=z[:, 0:nt], func=AF.Sqrt, bias=eps_c, scale=1.0 / dim),
                 reads=[zbuf], writes=[rb], extra=tconst)
            p.do('dve', lambda e, r=r: e.reciprocal(out=r[:, 0:nt], in_=r[:, 0:nt]), reads=[rb], writes=[rb])
            for m in range(nm):
                f, bf_ = cqf[slot0 + m], b_cqf[slot0 + m]
                p.do('pool' if m % 2 == 0 else 'dve',
                     lambda e, f=f, r=r, m=m: e.tensor_tensor(out=dst[:, m, tcol0:tcol0 + nt], in0=f[:, 0:nt], in1=r[:, 0:nt], op=ALU.mult),
                     reads=[bf_, rb], pwrites=[dstbuf])

        def groups_glu(ci):
            kind, t0, nt, cidx = chunks[ci]
            sl = ci % 2
            if kind != 'lat':
                return
            cs_ = ci % 2
            p.dma('sp', 'cst%d' % cs_, cst[cs_][0:32, 0, :], cos_d[0:32, t0:t0 + 512], writes=[b_cst[cs_]])
            p.dma('sp', 'cst%d' % cs_, cst[cs_][0:32, 1, :], sin_d[0:32, t0:t0 + 512], pwrites=[b_cst[cs_]])
            b_vT[cidx].begin()
            for c in range(4):
                za, zab = zgroup(sl, nt, c * 128, 128)
                zg, zgb = zgroup(sl, nt, 512 + c * 128, 128)
                tsg, tsb = tnext()
                p.do('act', lambda e, zg=zg, tsg=tsg: e.activation(out=tsg, in_=zg, func=AF.Sigmoid), reads=[zgb], writes=[tsb])
                p.do('dve', lambda e, za=za, tsg=tsg, c=c: e.tensor_tensor(out=vT[:, c, 15 + t0:15 + t0 + 512], in0=za, in1=tsg, op=ALU.mult),
                     reads=[zab, tsb], pwrites=[b_vT[cidx]])
                if ci + 1 < len(chunks) and c < chunks[ci + 1][2] // 128:
                    pro_stats(ci + 1, c)

        def groups_lat(ci):
            kind, t0, nt, cidx = chunks[ci]
            sl = ci % 2
            if kind == 'lat':
                rms_front(sl, nt, 1024, 3, 0)
            rms_front(sl, nt, 1408, 2, 3)
            kcol0 = t0 if kind == 'lat' else N
            b_krope[cidx].begin()
            za, zab = zgroup(sl, nt, 1664, 32)
            if kind == 'lat':
                zb_, zbb = zgroup(sl, nt, 1696, 32)
                t1_, t1b = tnext()
                t2_, t2b = tnext()
                cs_ = ci % 2
                p.do('dve', lambda e, za=za, t1_=t1_: e.tensor_tensor(out=t1_[0:32, :], in0=za[0:32, :], in1=cst[cs_][0:32, 0, :], op=ALU.mult),
                     reads=[zab, b_cst[cs_]], writes=[t1b])
                p.do('dve', lambda e, zb_=zb_, t2_=t2_: e.tensor_tensor(out=t2_[0:32, :], in0=zb_[0:32, :], in1=cst[cs_][0:32, 1, :], op=ALU.mult),
                     reads=[zbb, b_cst[cs_]], writes=[t2b])
                p.do('pool', lambda e, t1_=t1_, t2_=t2_: e.tensor_tensor(out=KT[64:96, 0, kcol0:kcol0 + nt], in0=t1_[0:32, :], in1=t2_[0:32, :], op=ALU.add),
                     reads=[t1b, t2b], pwrites=[b_krope[cidx]])
            else:
                p.do('act', lambda e, za=za: e.activation(out=KT[64:96, 0, kcol0:kcol0 + nt], in_=za[0:32, 0:nt], func=AF.Copy),
                     reads=[zab], pwrites=[b_krope[cidx]])
            p.do('pool', lambda e: e.tensor_copy(out=KT[64:96, 1, kcol0:kcol0 + nt], in_=KT[64:96, 0, kcol0:kcol0 + nt]),
                 reads=[b_krope[cidx]], pwrites=[b_krope[cidx]])

        def groups_fin(ci):
            kind, t0, nt, cidx = chunks[ci]
            if kind == 'lat':
                b_cqn[cidx].begin()
                rms_back(nt, 3, 0, 384.0, cqn, b_cqn[cidx], t0)
            b_ckvn[cidx].begin()
            kcol0 = t0 if kind == 'lat' else N
            rms_back(nt, 2, 3, 256.0, ckvn, b_ckvn[cidx], kcol0)

        nchunks = len(chunks)
        tile_base = {}
        tb_ = 0
        for ci_ in range(nchunks):
            tile_base[ci_] = tb_
            tb_ += chunks[ci_][2] // 128
        xload(0)
        for i_ in range(chunks[0][2] // 128):
            pro_stats(0, i_)
        pro_norm(0)
        pro_transposes(0)
        if nchunks > 1:
            xload(1)
        for ci in range(nchunks):
            groups_glu(ci)
            if ci >= 1:
                groups_fin(ci - 1)
            if ci + 1 < nchunks:
                if chunks[ci][0] != 'lat':
                    for i_ in range(chunks[ci + 1][2] // 128):
                        pro_stats(ci + 1, i_)
                pro_norm(ci + 1)
            groups_lat(ci)
            if ci + 1 < nchunks:
                pro_transposes(ci + 1)
                if ci + 2 < nchunks:
                    xload(ci + 2)
        groups_fin(nchunks - 1)
        early('a5')
        tokA = p.barrier()

        def dump(name, src_ap):
            dst = dbg_d[name]
            t = p.dma('sp', 'dbg', dst, src_ap)
            return t

        def finish(extra=()):
            p._waits('sp', _merge({}, extra))
            for k, v in p.cnt.items():
                if k not in p.ENGS and v > 0:
                    p._waits('sp', (k, v))
            p.run()

        if upto == 'A':
            A.release()
            A.mark()
            dtmp = A.alloc(NK * 4, F32)
            toks = []
            last = None
            for name, src in [('d_cqn0', cqn[:, 0, :]), ('d_cqn2', cqn[:, 2, :]), ('d_ckvn0', ckvn[:, 0, :]), ('d_ckvn1', ckvn[:, 1, :]),
                              ('d_v0', vT[:, 0, 0:N + 30]), ('d_v3', vT[:, 3, 0:N + 30]), ('d_krope', KT[:, 0, :]), ('d_krope1', KT[:, 1, :])]:
                if name not in dbg_d:
                    continue
                n = src.shape[1]
                np_ = src.shape[0]
                p.op('dve', lambda e: e.memset(dtmp, 0.0), waits=[last] if last else ())
                if name.startswith('d_krope'):
                    tk = p.op('dve', lambda e, src=src, n=n: e.tensor_copy(out=dtmp[64:96, 0:n], in_=src[64:96, :]), waits=[("dve", p.cnt["dve"])])
                else:
                    tk = p.op('dve', lambda e, src=src, n=n: e.tensor_copy(out=dtmp[:, 0:n], in_=src), waits=[("dve", p.cnt["dve"])])
                last = p.dma('sp', 'dbg', dbg_d[name], dtmp[:, 0:n], extra=[tk])
                p._waits('dve', [last])
            if 'd_dcol' in dbg_d:
                last = p.dma('sp', 'dbg2', dbg_d['d_dcol'], dcol)
            if 'd_stats' in dbg_d:
                last = p.dma('sp', 'dbg3', dbg_d['d_stats'], stats.rearrange("p t s -> p (t s)"))
            finish()
            return nc
        A.release()

        def dump_cat(items):
            A.mark()
            dtmp = A.alloc(N * 4, F32)
            last = None
            for name, c, rows in items:
                if name not in dbg_d:
                    continue
                tk = p.op('dve', lambda e, c=c: e.tensor_copy(out=dtmp[:, 0:N], in_=cat[:, c, 1:N + 1]),
                          waits=[last, ("dve", p.cnt["dve"])] if last else [("dve", p.cnt["dve"])])
                last = p.dma('sp', 'dbg', dbg_d[name], dtmp[:, 0:N], extra=[tk])
            A.release()

        A.mark()
        diag = A.alloc(124 * 128 * 2, BF16).rearrange("p (i c) -> p i c", i=124)
        b_diag = Buf()
        b_diag.begin()
        for idx in range(124):
            if idx % 2 == 0:
                p.do('dve', lambda e, idx=idx: e.tensor_scalar(out=diag[:, idx, :], in0=identF, scalar1=pv('conv_w', idx), scalar2=None, op0=ALU.mult),
                     reads=[b_pvec, b_const], pwrites=[b_diag])
            else:
                p.do('act', lambda e, idx=idx: e.activation(out=diag[:, idx, :], in_=identF, func=AF.Copy, scale=pv('conv_w', idx)),
                     reads=[b_pvec, b_const], pwrites=[b_diag])
        convf = [[A.alloc(2048, F32) for _ in range(4)] for _ in range(2)]
        b_convf = [[Buf() for _ in range(4)] for _ in range(2)]
        cbb = [A.alloc(1024, BF16) for _ in range(4)]
        csq_ = [A.alloc(1024, BF16) for _ in range(4)]
        b_cbb = [Buf() for _ in range(4)]
        b_csq = [Buf() for _ in range(4)]
        meanb = [A.alloc(2048, F32) for _ in range(2)]
        m2b = [A.alloc(2048, F32) for _ in range(2)]
        rstb = [A.alloc(2048, F32) for _ in range(2)]
        b_meanb, b_m2b, b_rstb = [Buf(), Buf()], [Buf(), Buf()], [Buf(), Buf()]
        ttmp = [A.alloc(2048, F32) for _ in range(2)]
        b_ttmp = [Buf(), Buf()]
        cbk = [Buf() for _ in range(4)]
        sbk = [Buf() for _ in range(4)]
        b_cat = [[Buf() for _ in range(NCH)] for _ in range(8)]
        tt_i = [0]
        for ci in range(NCH):
            t0 = ci * 512
            sl = ci % 2
            for c in range(4):
                z = bank(c)

                def mm(e, z=z, c=c, t0=t0):
                    ins = None
                    for k in range(31):
                        ins = e.matmul(z, lhsT=diag[:, c * 31 + k, :], rhs=vT[:, c, t0 + k:t0 + k + 512], start=(k == 0), stop=(k == 30))
                    return ins
                p.do('pe', mm, reads=[b_diag, b_vpad] + b_vT, writes=[cbk[c]])
                f = convf[sl][c]
                p.do('act', lambda e, z=z, f=f, c=c: e.activation(out=f, in_=z, func=AF.Identity, bias=pv('conv_b', c), scale=1.0),
                     reads=[cbk[c], b_pvec], writes=[b_convf[sl][c]])
                p.do('pool', lambda e, f=f, c=c: e.tensor_copy(out=cbb[c], in_=f), reads=[b_convf[sl][c]], writes=[b_cbb[c]])
                p.do('dve', lambda e, f=f, c=c: e.tensor_tensor(out=csq_[c], in0=f, in1=f, op=ALU.mult), reads=[b_convf[sl][c]], writes=[b_csq[c]])
            zm, ze = bank(4 + sl), bank(6 + sl)

            def mms(e, zm=zm):
                ins = None
                for c in range(4):
                    ins = e.matmul(zm, lhsT=onesB, rhs=cbb[c], start=(c == 0), stop=(c == 3))
                return ins

            def mme(e, ze=ze):
                ins = None
                for c in range(4):
                    ins = e.matmul(ze, lhsT=onesB, rhs=csq_[c], start=(c == 0), stop=(c == 3))
                return ins
            p.do('pe', mms, reads=b_cbb, writes=[sbk[sl]], extra=tconst)
            p.do('pe', mme, reads=b_csq, writes=[sbk[2 + sl]], extra=tconst)
            p.do('act', lambda e, zm=zm, sl=sl: e.activation(out=meanb[sl], in_=zm, func=AF.Copy, scale=1.0 / 512),
                 reads=[sbk[sl]], writes=[b_meanb[sl]])
            p.do('dve', lambda e, sl=sl: e.tensor_tensor(out=m2b[sl], in0=meanb[sl], in1=meanb[sl], op=ALU.mult),
                 reads=[b_meanb[sl]], writes=[b_m2b[sl]])
            p.do('dve', lambda e, ze=ze, sl=sl: e.scalar_tensor_tensor(out=rstb[sl], in0=ze, scalar=1.0 / 512, in1=m2b[sl], op0=ALU.mult, op1=ALU.subtract),
                 reads=[sbk[2 + sl], b_m2b[sl]], writes=[b_rstb[sl]])
            p.do('act', lambda e, sl=sl: e.activation(out=rstb[sl], in_=rstb[sl], func=AF.Sqrt, bias=eps_c, scale=1.0),
                 reads=[b_rstb[sl]], writes=[b_rstb[sl]], extra=tconst)
            p.do('dve', lambda e, sl=sl: e.reciprocal(out=rstb[sl], in_=rstb[sl]), reads=[b_rstb[sl]], writes=[b_rstb[sl]])
            for c in range(4):
                ti = tt_i[0] % 2
                tt_i[0] += 1
                f = convf[sl][c]
                p.do('dve', lambda e, f=f, ti=ti, sl=sl: e.tensor_tensor(out=ttmp[ti], in0=f, in1=meanb[sl], op=ALU.subtract),
                     reads=[b_convf[sl][c], b_meanb[sl]], writes=[b_ttmp[ti]])
                p.do('dve', lambda e, ti=ti, sl=sl: e.tensor_tensor(out=ttmp[ti], in0=ttmp[ti], in1=rstb[sl], op=ALU.mult),
                     reads=[b_ttmp[ti], b_rstb[sl]], writes=[b_ttmp[ti]])
                p.do('act', lambda e, ti=ti, c=c, t0=t0: e.activation(out=cat[:, c, 1 + t0:1 + t0 + 512], in_=ttmp[ti], func=AF.Silu,
                                                                     scale=pv('conv_g', c), bias=pv('conv_lb', c)),
                     reads=[b_ttmp[ti], b_pvec], writes=[b_cat[c][ci]])
        p.barrier()
        A.release()
        if upto == 'B':
            dump_cat([('d_conv0', 0, 128), ('d_conv3', 3, 128)])
            finish()
            return nc

        A.mark()
        VT = A.alloc(2 * NKT * 128 * 2, BF16).rearrange("p (s t) -> p s t", s=2)
        w_uq_bf = A.alloc(3 * 768 * 2, BF16).rearrange("p (k c) -> p k c", k=3)
        w_uqb_bf = A.alloc(3 * 256 * 2, BF16).rearrange("p (k c) -> p k c", k=3)
        w_ukv_bf = A.alloc(2 * 1024 * 2, BF16).rearrange("p (k c) -> p k c", k=2)
        wst = [A.alloc(1024 * 4, F32) for _ in range(2)]
        b_wst = [Buf(), Buf()]
        QT = A.alloc(2 * 512 * 2, BF16).rearrange("p (s t) -> p s t", s=2)
        b_QT = [Buf(), Buf()]
        NSS = 3
        PT = A.alloc(NSS * 1024 * 2, BF16).rearrange("p (s t) -> p s t", s=NSS)
        b_PT = [Buf() for _ in range(NSS)]
        Osb = [A.alloc(2048, F32) for _ in range(2)]
        b_Osb = [Buf(), Buf()]
        csq = A.alloc(2 * 2 * 512 * 4, F32).rearrange("p (s a t) -> p s a t", s=2, a=2)
        b_csq2 = [Buf(), Buf()]
        tq = [A.alloc(2048, F32) for _ in range(4)]
        b_tq = [Buf() for _ in range(4)]
        Rt = [A.alloc(2048, F32) for _ in range(2)]
        b_Rt = [Buf(), Buf()]
        nqg = A.alloc(3 * 4, F32)
        b_w = Buf()
        b_VT = [Buf(), Buf()]
        b_KT = [Buf(), Buf()]
        b_nqg = Buf()
        p.do('dve', lambda e: e.tensor_scalar(out=nqg, in0=pv('qg'), scalar1=-1.0, scalar2=None, op0=ALU.mult), reads=[b_pvec], writes=[b_nqg])
        b_w.begin()
        for k in range(3):
            st_ = wst[k % 2][:, 0:768]
            p.dma('sp', 'wst%d' % (k % 2), st_, wuq_d[k * 128:(k + 1) * 128, :], writes=[b_wst[k % 2]])
            p.do('dve', lambda e, k=k, st_=st_: e.tensor_scalar(out=w_uq_bf[:, k, :], in0=st_, scalar1=pv('qg', k), scalar2=None, op0=ALU.mult),
                 reads=[b_wst[k % 2], b_pvec], pwrites=[b_w])
            st3 = st_.rearrange("p (h d) -> p h d", h=8)
            ob = w_uqb_bf[:, k, :].rearrange("p (h d) -> p h d", h=8)
            p.do('dve', lambda e, k=k, st3=st3, ob=ob: e.tensor_scalar(out=ob[:, :, 0:16], in0=st3[:, :, 80:96], scalar1=nqg[:, k:k + 1], scalar2=None, op0=ALU.mult),
                 reads=[b_wst[k % 2], b_nqg], pwrites=[b_w])
            p.do('dve', lambda e, k=k, st3=st3, ob=ob: e.tensor_scalar(out=ob[:, :, 16:32], in0=st3[:, :, 64:80], scalar1=pv('qg', k), scalar2=None, op0=ALU.mult),
                 reads=[b_wst[k % 2], b_pvec], pwrites=[b_w])
        for k in range(2):
            st_ = wst[(k + 1) % 2]
            p.dma('sp', 'wst%d' % ((k + 1) % 2), st_, wukv_d[k * 128:(k + 1) * 128, :], writes=[b_wst[(k + 1) % 2]])
            p.do('dve', lambda e, k=k, st_=st_: e.tensor_scalar(out=w_ukv_bf[:, k, :], in0=st_, scalar1=pv('kvg', k), scalar2=None, op0=ALU.mult),
                 reads=[b_wst[(k + 1) % 2], b_pvec], pwrites=[b_w])
        VT4 = VT.rearrange("p s (t c) -> p s t c", c=128)
        for s_ in range(2):
            b_VT[s_].begin()
            p.do('pool', lambda e, s_=s_: e.memset(VT4[:, s_, :, 64:128], 1.0), pwrites=[b_VT[s_]])
        Sb = [Buf() for _ in range(NSS)]
        Ob = [Buf()]
        Mb = [Buf()]

        Mfree = [Buf() for _ in range(7)]
        mring = {'banks': None, 'i': 0}

        def mnext():
            if mring['banks']:
                bk = mring['banks'][mring['i'] % len(mring['banks'])]
                mring['i'] += 1
                return bank(bk), (Sb[bk // 2] if False else Mfree[bk])
            return bank(7), Mb[0]

        all_ckvn = b_ckvn
        all_krope = b_krope
        KCH = [(c0, min(512, NK - c0)) for c0 in range(0, NK, 512)]

        def kv_tasks(h):
            hs = h % 2
            tasks = []

            def start():
                b_KT[hs].begin()
                b_VT[hs].begin()
            tasks.append(start)
            for (c0, n) in KCH:
                def tk(c0=c0, n=n):
                    z, zb_ = mnext()

                    def mm(e):
                        ins = None
                        for k in range(2):
                            ins = e.matmul(z[0:64, 0:n], lhsT=w_ukv_bf[:, k, 128 * h:128 * h + 64], rhs=ckvn[:, k, c0:c0 + n], start=(k == 0), stop=(k == 1))
                        return ins
                    p.do('pe', mm, reads=[b_w] + all_ckvn, writes=[zb_])
                    p.do('dve', lambda e: e.tensor_copy(out=KT[0:64, hs, c0:c0 + n], in_=z[0:64, 0:n]), reads=[zb_], pwrites=[b_KT[hs]])
                tasks.append(tk)
            for t0_ in range(0, NKT, 8):
                def tv(t0_=t0_):
                    nt_ = min(8, NKT - t0_)
                    z, zb_ = mnext()

                    def mm(e):
                        ins = None
                        for j in range(nt_):
                            kt = t0_ + j
                            for k in range(2):
                                ins = e.matmul(z[:, j * 64:(j + 1) * 64], lhsT=ckvn[:, k, kt * 128:(kt + 1) * 128],
                                               rhs=w_ukv_bf[:, k, 128 * h + 64:128 * h + 128], start=(k == 0), stop=(k == 1))
                        return ins
                    p.do('pe', mm, reads=[b_w] + all_ckvn, writes=[zb_])
                    p.do('dve', lambda e: e.tensor_copy(out=VT4[:, hs, t0_:t0_ + nt_, 0:64],
                                                        in_=z[:, 0:nt_ * 64].rearrange("p (t c) -> p t c", c=64)),
                         reads=[zb_], pwrites=[b_VT[hs]])
                tasks.append(tv)
            return tasks

        qcount = [0]

        def q_gen_tasks(h, qc, holder):
            def ta():
                qs = qcount[0] % 2
                qcount[0] += 1
                holder['qs'] = qs
                t0 = qc * 512
                p.dma('sp', 'csq%d' % qs, csq[64:96, qs, 0, :], cos_d[64:96, t0:t0 + 512], writes=[b_csq2[qs]])
                p.dma('sp', 'csq%d' % qs, csq[0:32, qs, 1, :], sin_d[0:32, t0:t0 + 512], pwrites=[b_csq2[qs]])
                za, zab = mnext()

                def mma(e):
                    ins = None
                    for k in range(3):
                        ins = e.matmul(za[0:96, :], lhsT=w_uq_bf[:, k, 96 * h:96 * h + 96], rhs=cqn[:, k, t0:t0 + 512], start=(k == 0), stop=(k == 2))
                    return ins
                p.do('pe', mma, reads=[b_w] + b_cqn, writes=[zab])
                b_QT[qs].begin()
                t1_, t1b = tq[2 * qs], b_tq[2 * qs]
                p.do('dve', lambda e: e.tensor_copy(out=QT[0:64, qs, :], in_=za[0:64, :]), reads=[zab], pwrites=[b_QT[qs]])
                p.do('dve', lambda e: e.tensor_tensor(out=t1_[64:96, :], in0=za[64:96, :], in1=csq[64:96, qs, 0, :], op=ALU.mult),
                     reads=[zab, b_csq2[qs]], writes=[t1b])

            def tb():
                qs = holder['qs']
                t0 = qc * 512
                t1_, t1b = tq[2 * qs], b_tq[2 * qs]
                t2_, t2b = tq[2 * qs + 1], b_tq[2 * qs + 1]
                zq, zqb = mnext()

                def mmb(e):
                    ins = None
                    for k in range(3):
                        ins = e.matmul(zq[0:32, :], lhsT=w_uqb_bf[:, k, 32 * h:32 * h + 32], rhs=cqn[:, k, t0:t0 + 512], start=(k == 0), stop=(k == 2))
                    return ins
                p.do('pe', mmb, reads=[b_w] + b_cqn, writes=[zqb])
                p.do('dve', lambda e: e.tensor_tensor(out=t2_[64:96, :], in0=zq[0:32, :], in1=csq[0:32, qs, 1, :], op=ALU.mult),
                     reads=[zqb, b_csq2[qs]], writes=[t2b])
                p.do('dve', lambda e: e.tensor_tensor(out=QT[64:96, qs, :], in0=t1_[64:96, :], in1=t2_[64:96, :], op=ALU.add),
                     reads=[t1b, t2b], pwrites=[b_QT[qs]])
            return [ta, tb]

        NB = (NKT + 1) // 2
        ocount = [0]
        scount = [0]
        pcount = [0]

        norm_prev = [None]

        def attention(h, qc, qs, pre_tasks, carry):
            carry_used = [bool(carry)]
            hs = h % 2
            os_ = 0
            osb = ocount[0] % 2
            ocount[0] += 1
            O = bank(6)
            pend = []

            def qk(b):
                ss = scount[0] % NSS
                scount[0] += 1
                S = bank(2 * ss, 2)
                nt_ = min(2, NKT - 2 * b)

                def mm(e):
                    ins = None
                    for j in range(nt_):
                        kt = 2 * b + j
                        ins = e.matmul(S[:, j * 512:(j + 1) * 512], lhsT=KT[0:96, hs, kt * 128:(kt + 1) * 128], rhs=QT[0:96, qs, :], start=True, stop=True)
                    return ins
                p.do('pe', mm, reads=[b_KT[hs], b_QT[qs]] + all_krope, writes=[Sb[ss]])
                ps_ = pcount[0] % NSS
                pcount[0] += 1
                p.do('act', lambda e: e.activation(out=PT[:, ps_, 0:nt_ * 512], in_=S[:, 0:nt_ * 512], func=AF.Exp, scale=SCALE),
                     reads=[Sb[ss]], writes=[b_PT[ps_]])
                pend.append((b, ps_, nt_))

            def pv_(b, ps_, nt_):
                def mm(e):
                    ins = None
                    for j in range(nt_):
                        kt = 2 * b + j
                        ins = e.matmul(O, lhsT=VT[:, hs, kt * 128:(kt + 1) * 128], rhs=PT[:, ps_, j * 512:(j + 1) * 512],
                                       start=(kt == 0), stop=(kt == NKT - 1))
                    return ins
                if b == 0:
                    p.do('pe', mm, reads=[b_VT[hs], b_PT[ps_]], writes=[Ob[os_]])
                else:
                    p.do('pe', mm, reads=[b_VT[hs], b_PT[ps_]], pwrites=[Ob[os_]])

            ptasks = list(pre_tasks)
            rs = osb

            def normalize():
                p.do('dve', lambda e: e.reciprocal(out=Rt[rs][0:64, :], in_=Osb[osb][64:128, :]), reads=[b_Osb[osb]], writes=[b_Rt[rs]])
                pb = (h % 2) * 64
                p.do('dve', lambda e: e.tensor_tensor(out=cat[pb:pb + 64, 4 + h // 2, 1 + qc * 512:1 + qc * 512 + 512], in0=Osb[osb][0:64, :], in1=Rt[rs][0:64, :], op=ALU.mult),
                     reads=[b_Osb[osb], b_Rt[rs]], writes=[b_cat[4 + h // 2][qc] if h % 2 == 0 else Buf()])

            def finish():
                if norm_prev[0] is not None:
                    norm_prev[0]()
                p.do('dve', lambda e: e.tensor_copy(out=Osb[osb], in_=O), reads=[Ob[os_]], writes=[b_Osb[osb]])
                norm_prev[0] = normalize

            for b in range(NB):
                qk(b)
                if b >= 2 and (b - 2) % 3 == 0 and ptasks:
                    ptasks.pop(0)()
                if b == 7 and norm_prev[0] is not None:
                    norm_prev[0]()
                    norm_prev[0] = None
                if carry:
                    carry.pop(0)()
                elif b >= 2 or not carry_used[0]:
                    if pend and (b >= 2):
                        pv_(*pend.pop(0))
            while ptasks:
                ptasks.pop(0)()
            while len(pend) > 2:
                pv_(*pend.pop(0))
            items = list(pend)
            pend.clear()
            left = [lambda it_=it_: pv_(*it_) for it_ in items[:-1]]
            last_ = items[-1]
            left.append(lambda: (pv_(*last_), finish()))
            return left

        b_scr = Buf()
        b_scr.begin()
        prep_steps = []
        if upto not in ('C1',):
            pst_f = A.alloc(4096, F32)
            pst_b = [A.alloc(2048, BF16) for _ in range(2)]
            b_pf, b_pb = Buf(), [Buf(), Buf()]
            wv = wups_d.rearrange("g p (k c) -> g p k c", k=8)
            pieces = []
            for k in range(8):
                for half in range(2):
                    for (c0_, nc_) in ((0, 1024), (1024, 1024), (2048, 768)):
                        g0_ = c0_ // 256
                        ng_ = nc_ // 256
                        pieces.append((wup_d[k * 128:(k + 1) * 128, half * 2816 + c0_:half * 2816 + c0_ + nc_], nc_,
                                       (lambda b_, k=k, half=half, g0_=g0_, ng_=ng_: (
                                           wv[g0_:g0_ + ng_, :, k, half * 256:(half + 1) * 256].rearrange("g p c -> p g c"),
                                           b_.rearrange("p (g c) -> p g c", c=256)))))
            for m in range(MFF):
                pieces.append((wdn_d[m * 128:(m + 1) * 128, :], 1024, (lambda b_, m=m: (wdns_d[:, m * 1024:(m + 1) * 1024], b_))))

            def stage_in(i):
                src_ap, ncols, dst_fn = pieces[i]
                p.dma('sp', 'ppf', pst_f[:, 0:ncols], src_ap, writes=[b_pf])
                p.do('dve', lambda e: e.tensor_copy(out=pst_b[i % 2][:, 0:ncols], in_=pst_f[:, 0:ncols]), reads=[b_pf], writes=[b_pb[i % 2]])

            def stage_out(i):
                src_ap, ncols, dst_fn = pieces[i]
                dst, srcv = dst_fn(pst_b[i % 2][:, 0:ncols])
                p.dma('sp', 'ppb%d' % (i % 2), dst, srcv, reads=[b_pb[i % 2]], pwrites=[b_scr])

            npc = len(pieces)
            nsteps = NH * NCH
            extra_ = [max(0, npc - nsteps)]
            stage_in(0)
            pi_ = [1]

            def prep_step():
                reps = 1
                if extra_[0] > 0:
                    reps = 2
                    extra_[0] -= 1
                for _ in range(reps):
                    i = pi_[0]
                    if i - 1 < npc and i - 1 >= 0 and i <= npc:
                        stage_out(i - 1)
                    if i < npc:
                        stage_in(i)
                    pi_[0] += 1
            prep_steps.append(prep_step)
        mring['banks'] = [0, 1, 2, 3, 4, 5, 6]
        for tsk in kv_tasks(0):
            tsk()
        mring['banks'] = None
        for bk in range(6):
            _merge(Sb[bk // 2].rd, Mfree[bk].rd)
            _merge(Sb[bk // 2].wr, Mfree[bk].wr)
        _merge(Ob[0].rd, Mfree[6].rd)
        _merge(Ob[0].wr, Mfree[6].wr)
        order = [(h, qc) for h in range(NH) for qc in range(NCH)]
        holder = {}
        carry_ = []
        for tsk in q_gen_tasks(0, 0, holder):
            tsk()
        for idx, (h, qc) in enumerate(order):
            pre = []
            nholder = {}
            if idx + 1 < len(order):
                hn, qn = order[idx + 1]
                pre.extend(q_gen_tasks(hn, qn, nholder))
            if h + 1 < NH:
                tks = kv_tasks(h + 1)
                per = (len(tks) + NCH - 1) // NCH
                pre.extend(tks[qc * per:(qc + 1) * per])
            if prep_steps:
                prep_steps[0]()
            carry_ = attention(h, qc, holder['qs'], pre, carry_)
            holder = nholder
        for f_ in carry_:
            f_()
        if norm_prev[0] is not None:
            norm_prev[0]()
            norm_prev[0] = None
        if prep_steps:
            while pi_[0] <= npc:
                prep_steps[0]()
        p.barrier()
        A.release()
        if upto == 'C1':
            dump_cat([('d_attn0', 4, 128), ('d_attn3', 7, 128)])
            finish()
            return nc

        if off_dead1 - off_dead0 >= 59000:
            A2 = Arena(arena_t[:, off_dead0:off_dead1], off_dead1 - off_dead0)
        else:
            A2 = Arena(A.raw(59392), 59392)
        A.mark()
        A2.mark()
        w_o_bf = A.alloc(8 * 1024 * 2, BF16).rearrange("p (k c) -> p k c", k=8)
        g1bc = A.alloc(4096, F32)
        statsA = A.alloc(40 * 2 * 4, F32).rearrange("p (t s) -> p t s", s=2)
        rT0 = A.alloc(8 * 2048, F32).rearrange("p (j t) -> p j t", j=8)
        rbb = [A.alloc(1024, BF16) for _ in range(4)]
        rsq = [A.alloc(1024, BF16) for _ in range(4)]
        b_rbb = [Buf() for _ in range(4)]
        b_rsq = [Buf() for _ in range(4)]
        meanc = [A.alloc(2048, F32) for _ in range(2)]
        m2c = [A.alloc(2048, F32) for _ in range(2)]
        rstc = [A.alloc(2048, F32) for _ in range(2)]
        b_meanc, b_m2c, b_rstc = [Buf(), Buf()], [Buf(), Buf()], [Buf(), Buf()]
        xst = [A.alloc(2048, F32) for _ in range(3)]
        b_xst = [Buf() for _ in range(3)]
        wst2 = [A.alloc(4096, F32) for _ in range(2)]
        b_wst2 = [Buf(), Buf()]
        xr = [A2.alloc(4096, F32) for _ in range(8)]
        b_xr = [Buf() for _ in range(8)]
        rT1 = A2.alloc(8 * 2048, F32).rearrange("p (j t) -> p j t", j=8)
        rTs = [rT0, rT1]
        b_rTs = [[Buf() for _ in range(8)] for _ in range(2)]
        b_wo, b_g1bc, b_statsA = Buf(), Buf(), Buf()
        p.do('dve', lambda e: e.tensor_scalar(out=statsA.rearrange("p t s -> p (t s)"), in0=stats.rearrange("p t s -> p (t s)"),
                                              scalar1=ALPHA, scalar2=None, op0=ALU.mult), reads=[b_stats], writes=[b_statsA])
        dg = A2.alloc(8 * 512, F32).rearrange("p (j c) -> p j c", j=8)
        b_dg = Buf()
        b_dg.begin()
        for j in range(8):
            p.do('dve', lambda e, j=j: e.tensor_scalar(out=dg[:, j, :], in0=identF, scalar1=modL(2, j), scalar2=None, op0=ALU.mult),
                 reads=[b_dcol, b_const], pwrites=[b_dg])
        gb_b = [Buf(), Buf()]
        for hb in range(2):
            def mm(e, hb=hb):
                ins = None
                for jj in range(4):
                    j = hb * 4 + jj
                    ins = e.matmul(bank(hb)[:, jj * 128:(jj + 1) * 128], lhsT=onesF, rhs=dg[:, j, :], start=True, stop=True)
                return ins
            p.do('pe', mm, reads=[b_dg], writes=[gb_b[hb]], extra=tconst)
            if hb == 0:
                p.do('act', lambda e, hb=hb: e.activation(out=g1bc[:, hb * 512:(hb + 1) * 512], in_=bank(hb), func=AF.Copy), reads=[gb_b[hb]], writes=[b_g1bc])
            else:
                p.do('act', lambda e, hb=hb: e.activation(out=g1bc[:, hb * 512:(hb + 1) * 512], in_=bank(hb), func=AF.Copy), reads=[gb_b[hb]], pwrites=[b_g1bc])
        b_wo.begin()
        for k in range(8):
            st_ = wst2[k % 2]
            p.dma('sp', 'wst%d' % (k % 2), st_, wo_d[k * 128:(k + 1) * 128, :], writes=[b_wst2[k % 2]])
            p.do('dve', lambda e, k=k, st_=st_: e.tensor_tensor(out=w_o_bf[:, k, :], in0=st_, in1=g1bc, op=ALU.mult),
                 reads=[b_wst2[k % 2], b_g1bc], pwrites=[b_wo])
        ybk = [Buf() for _ in range(4)]
        stb = [Buf() for _ in range(4)]
        xrc = [0]
        ri = [0]
        xsi = [0]
        xs_of = {}

        def c2a_xload(ci):
            t0 = ci * 512
            xs_ = []
            for i in range(4):
                xi = xrc[0] % 8
                xrc[0] += 1
                p.dma('sp', 'xr%d' % xi, xr[xi], x_d[t0 + i * 128:t0 + (i + 1) * 128, :], writes=[b_xr[xi]])
                xs_.append(xi)
            xs_of[ci] = xs_

        def c2a_xscale(ci):
            for i in range(4):
                xi = xs_of[ci][i]
                gt = ci * 4 + i
                p.do('act', lambda e, xi=xi, gt=gt: e.activation(out=xr[xi], in_=xr[xi], func=AF.Identity, scale=statsA[:, gt, 0:1], bias=statsA[:, gt, 1:2]),
                     reads=[b_xr[xi], b_statsA], writes=[b_xr[xi]])

        c2a_carry = []

        def c2a_front(ci):
            t0 = ci * 512
            sl = ci % 2
            rT = rTs[sl]
            b_rT = b_rTs[sl]
            if ci + 1 < NCH:
                c2a_xload(ci + 1)
            xs_ = xs_of[ci]
            zsum, zsq = bank(4 + sl), bank(6 + sl)
            pending = []

            def stat_mm(j, rb_i, first, last):
                p.do('pe', lambda e: e.matmul(zsum, lhsT=onesB, rhs=rbb[rb_i], start=first, stop=last), reads=[b_rbb[rb_i]],
                     writes=[stb[sl]] if first else (), pwrites=() if first else [stb[sl]], extra=tconst)
                p.do('pe', lambda e: e.matmul(zsq, lhsT=onesB, rhs=rsq[rb_i], start=first, stop=last), reads=[b_rsq[rb_i]],
                     writes=[stb[2 + sl]] if first else (), pwrites=() if first else [stb[2 + sl]], extra=tconst)

            for j in range(8):
                z = bank(j % 4)

                def mm(e, z=z, j=j):
                    ins = None
                    for k in range(8):
                        ins = e.matmul(z, lhsT=w_o_bf[:, k, j * 128:(j + 1) * 128], rhs=cat[:, k, 1 + t0:1 + t0 + 512], start=(k == 0), stop=False)
                    for i in range(4):
                        ins = e.matmul(z[:, i * 128:(i + 1) * 128], lhsT=xr[xs_[i]][:, j * 128:(j + 1) * 128], rhs=identF, start=False, stop=(i == 3))
                    return ins
                p.do('pe', mm, reads=[b_wo] + [b_cat[k][ci] for k in range(8)] + [b_xr[x] for x in xs_], writes=[ybk[j % 4]], extra=tconst)
                p.do('act', lambda e, z=z, j=j: e.activation(out=rT[:, j, :], in_=z, func=AF.Identity, bias=dcv('g1bo', j), scale=1.0),
                     reads=[ybk[j % 4], b_dcol], writes=[b_rT[j]])
                r_i = ri[0] % 4
                ri[0] += 1
                p.do('act', lambda e, j=j, r_i=r_i: e.activation(out=rbb[r_i], in_=rT[:, j, :], func=AF.Copy), reads=[b_rT[j]], writes=[b_rbb[r_i]])
                p.do('dve', lambda e, j=j, r_i=r_i: e.tensor_tensor(out=rsq[r_i], in0=rT[:, j, :], in1=rT[:, j, :], op=ALU.mult), reads=[b_rT[j]], writes=[b_rsq[r_i]])
                pending.append((j, r_i))
                if len(pending) > 2:
                    jj, rr = pending.pop(0)
                    stat_mm(jj, rr, jj == 0, False)
                if j == 1:
                    while c2a_carry:
                        c2a_carry.pop(0)()
                    if ci >= 1:
                        c2a_back_stats(ci - 1)
                if ci >= 1 and j >= 2:
                    c2a_back_j(ci - 1, j - 2)
            if ci >= 1:
                c2a_back_j(ci - 1, 6)
                c2a_back_j(ci - 1, 7)
            for (jj, rr) in pending:
                c2a_carry.append(lambda jj=jj, rr=rr: stat_mm(jj, rr, jj == 0, jj == 7))
            if ci + 1 < NCH:
                c2a_xscale(ci + 1)

        def c2a_back_stats(ci):
            sl = ci % 2
            zsum, zsq = bank(4 + sl), bank(6 + sl)
            p.do('act', lambda e, sl=sl: e.activation(out=meanc[sl], in_=zsum, func=AF.Copy, scale=1.0 / 1024), reads=[stb[sl]], writes=[b_meanc[sl]])
            p.do('dve', lambda e, sl=sl: e.tensor_tensor(out=m2c[sl], in0=meanc[sl], in1=meanc[sl], op=ALU.mult), reads=[b_meanc[sl]], writes=[b_m2c[sl]])
            p.do('dve', lambda e, sl=sl: e.scalar_tensor_tensor(out=rstc[sl], in0=zsq, scalar=1.0 / 1024, in1=m2c[sl], op0=ALU.mult, op1=ALU.subtract),
                 reads=[stb[2 + sl], b_m2c[sl]], writes=[b_rstc[sl]])
            p.do('act', lambda e, sl=sl: e.activation(out=rstc[sl], in_=rstc[sl], func=AF.Sqrt, bias=eps_c, scale=1.0), reads=[b_rstc[sl]], writes=[b_rstc[sl]], extra=tconst)
            p.do('dve', lambda e, sl=sl: e.reciprocal(out=rstc[sl], in_=rstc[sl]), reads=[b_rstc[sl]], writes=[b_rstc[sl]])

        def c2a_back_j(ci, j):
            t0 = ci * 512
            sl = ci % 2
            rT = rTs[sl]
            b_rT = b_rTs[sl]
            p.do('dve', lambda e: e.tensor_tensor(out=rT[:, j, :], in0=rT[:, j, :], in1=meanc[sl], op=ALU.subtract),
                 reads=[b_rT[j], b_meanc[sl]], writes=[b_rT[j]])
            p.do('dve', lambda e: e.tensor_tensor(out=rT[:, j, :], in0=rT[:, j, :], in1=rstc[sl], op=ALU.mult),
                 reads=[b_rT[j], b_rstc[sl]], writes=[b_rT[j]])
            p.do('act', lambda e: e.activation(out=cat[:, j, 1 + t0:1 + t0 + 512], in_=rT[:, j, :], func=AF.Identity, scale=dcv('G2', j), bias=dcv('B2', j)),
                 reads=[b_rT[j], b_dcol], writes=[b_cat[j][ci]])
            x_i = xsi[0] % 3
            xsi[0] += 1
            p.do('dve', lambda e: e.tensor_scalar(out=xst[x_i], in0=rT[:, j, :], scalar1=dcv('A3', j), scalar2=dcv('B3', j), op0=ALU.mult, op1=ALU.add),
                 reads=[b_rT[j], b_dcol], writes=[b_xst[x_i]])
            p.dma('sp', 'xst%d' % x_i, xts_d[j, :, t0:t0 + 512], xst[x_i], reads=[b_xst[x_i]])

        c2a_xload(0)
        c2a_xscale(0)
        for ci in range(NCH):
            c2a_front(ci)
        while c2a_carry:
            c2a_carry.pop(0)()
        c2a_back_stats(NCH - 1)
        for j in range(8):
            c2a_back_j(NCH - 1, j)
        p.barrier()
        A.release()
        A2.release()
        if upto == 'C2a':
            dump_cat([('d_h2_0', 0, 128), ('d_h2_7', 7, 128)])
            p.barrier()
            A.mark()
            dt2 = A.alloc(N * 4, F32)
            tk = p.dma('sp', 'dbg4', dt2, xts_d[3, :, :])
            p.dma('sp', 'dbg4', dbg_d['d_xt3'], dt2, extra=[tk])
            finish()
            return nc

        A.mark()
        A2.mark()
        wdn_bf = A.alloc(MFF * 1024 * 2, BF16).rearrange("p (m c) -> p m c", m=MFF)
        b_wdn = Buf()
        NWUG = 3
        wug = [A.alloc(8 * 512 * 2, BF16).rearrange("p (k c) -> p k c", k=8) for _ in range(NWUG)]
        b_wug = [Buf() for _ in range(NWUG)]
        aT = A2.alloc(MFF * 512 * 2, BF16).rearrange("p (m t) -> p m t", m=MFF)
        b_aT = [Buf() for _ in range(MFF)]
        cg = [A2.alloc(2048, F32) for _ in range(2)]
        cv = [A2.alloc(2048, F32) for _ in range(2)]
        sgb = [A2.alloc(2048, F32) for _ in range(2)]
        b_cg, b_cv, b_sg = [Buf(), Buf()], [Buf(), Buf()], [Buf(), Buf()]
        r2T = A2.alloc(8 * 2048, F32).rearrange("p (j t) -> p j t", j=8)
        b_r2 = [Buf() for _ in range(8)]
        ost = [A2.alloc(4096, F32) for _ in range(2)]
        b_ost = [Buf(), Buf()]
        rbb2 = [A.alloc(1024, BF16) for _ in range(2)]
        rsq2 = [A.alloc(1024, BF16) for _ in range(2)]
        b_rbb2 = [Buf() for _ in range(2)]
        b_rsq2 = [Buf() for _ in range(2)]
        mean2 = [A.alloc(2048, F32) for _ in range(1)] * 2
        m22 = [A.alloc(2048, F32) for _ in range(1)] * 2
        rst2 = [A.alloc(2048, F32) for _ in range(1)] * 2
        b_mean2, b_m22, b_rst2 = [Buf()] * 2, [Buf()] * 2, [Buf()] * 2
        b_h2 = Buf()
        p.do('pool', lambda e: e.memset(cat[:, :, 0:1], 0.0), writes=[b_h2])
        p.do('pool', lambda e: e.memset(cat[:, :, N + 1:N + 2], 0.0), pwrites=[b_h2])
        nchk = (N + 455) // 456
        base = N // nchk
        rem = N - base * nchk
        ranges = []
        s0 = 0
        for c_ in range(nchk):
            w_ = base + (1 if c_ < rem else 0)
            ranges.append((s0, s0 + w_))
            s0 += w_
        ubk = [Buf() for _ in range(4)]
        y2bk = [Buf(), Buf()]
        st2b = [Buf(), Buf()]
        gi = [0]
        ui = [0]
        ci2 = [0]
        xli = [0]
        r2i = [0]
        oi = [0]
        FW = PV['ffn_w'][0]

        def fw(m, tap):
            return pvec[:, FW + m * 3 + tap:FW + m * 3 + tap + 1]

        def xt_prefetch(rc):
            s_, e_ = ranges[rc]
            w_ = e_ - s_
            for j in range(8):
                p.dma('sp', 'xld%d' % j, r2T[:, j, 0:w_], xts_d[j, :, s_:e_], writes=[b_r2[j]])

        wq = {'issued': 0}
        NGRP = MFF // 2
        TOTG = NGRP * nchk

        def wug_issue(upto_g):
            while wq['issued'] < min(upto_g, TOTG):
                gq = wq['issued']
                g_ = gq % NWUG
                p.dma('sp', 'wug%d' % g_, wug[g_].rearrange("p k c -> p (k c)"), wups_d[gq % NGRP], reads=[b_scr], writes=[b_wug[g_]])
                wq['issued'] += 1

        def up_pairs(rc, m_list, state):
            s_, e_ = ranges[rc]
            w_ = e_ - s_
            for m in m_list:
                if m % 2 == 0:
                    gq = rc * NGRP + m // 2
                    wug_issue(gq + NWUG)
                    state['g'] = gq % NWUG
                g_ = state['g']
                zs = []
                for part in range(2):
                    u_ = ui[0] % 4
                    ui[0] += 1
                    z = bank(u_)
                    c0 = part * 256 + (m % 2) * 128

                    def mm(e, z=z, c0=c0, g_=g_):
                        ins = None
                        for k in range(8):
                            ins = e.matmul(z[:, 0:w_ + 2], lhsT=wug[g_][:, k, c0:c0 + 128], rhs=cat[:, k, s_:e_ + 2], start=(k == 0), stop=(k == 7))
                        return ins
                    p.do('pe', mm, reads=[b_wug[g_], b_h2] + [b_cat[k][cc] for k in range(8) for cc in range(NCH)], writes=[ubk[u_]])
                    zs.append((z, ubk[u_]))
                c_i = ci2[0] % 2
                ci2[0] += 1
                for part, (dst, bdst) in enumerate([(cg[c_i], b_cg[c_i]), (cv[c_i], b_cv[c_i])]):
                    z, zb_ = zs[part]
                    mm_ = m + part * MFF
                    p.do('act', lambda e, z=z, dst=dst, mm_=mm_: e.activation(out=dst[:, 0:w_], in_=z[:, 1:w_ + 1], func=AF.Identity, scale=fw(mm_, 1),
                                                                          bias=pv('ffn_b', mm_)), reads=[zb_, b_pvec], writes=[bdst])
                    p.do('dve', lambda e, z=z, dst=dst, mm_=mm_: e.scalar_tensor_tensor(out=dst[:, 0:w_], in0=z[:, 0:w_], scalar=fw(mm_, 0), in1=dst[:, 0:w_],
                                                                                op0=ALU.mult, op1=ALU.add), reads=[zb_, bdst, b_pvec], writes=[bdst])
                    p.do('dve', lambda e, z=z, dst=dst, mm_=mm_: e.scalar_tensor_tensor(out=dst[:, 0:w_], in0=z[:, 2:w_ + 2], scalar=fw(mm_, 2), in1=dst[:, 0:w_],
                                                                                op0=ALU.mult, op1=ALU.add), reads=[zb_, bdst, b_pvec], writes=[bdst])
                p.do('act', lambda e, c_i=c_i: e.activation(out=sgb[c_i][:, 0:w_], in_=cg[c_i][:, 0:w_], func=AF.Silu), reads=[b_cg[c_i]], writes=[b_sg[c_i]])
                p.do('pool', lambda e, c_i=c_i, m=m: e.tensor_tensor(out=aT[:, m, 0:w_], in0=sgb[c_i][:, 0:w_], in1=cv[c_i][:, 0:w_], op=ALU.mult),
                     reads=[b_sg[c_i], b_cv[c_i]], writes=[b_aT[m]])

        def down_and_ln(rc, mid=None):
            s_, e_ = ranges[rc]
            w_ = e_ - s_
            zsum, zsq = bank(6), bank(7)
            pending = []

            def stat_mm(jj, rr, first, last):
                p.do('pe', lambda e: e.matmul(zsum[:, 0:w_], lhsT=onesB, rhs=rbb2[rr][:, 0:w_], start=first, stop=last), reads=[b_rbb2[rr]],
                     writes=[st2b[0]] if first else (), pwrites=() if first else [st2b[0]], extra=tconst)
                p.do('pe', lambda e: e.matmul(zsq[:, 0:w_], lhsT=onesB, rhs=rsq2[rr][:, 0:w_], start=first, stop=last), reads=[b_rsq2[rr]],
                     writes=[st2b[1]] if first else (), pwrites=() if first else [st2b[1]], extra=tconst)

            for j in range(8):
                yb = j % 2
                z = bank(4 + yb)

                def mm(e, z=z, j=j):
                    ins = None
                    for m in range(MFF):
                        ins = e.matmul(z[:, 0:w_], lhsT=wdn_bf[:, m, j * 128:(j + 1) * 128], rhs=aT[:, m, 0:w_], start=(m == 0), stop=(m == MFF - 1))
                    return ins
                p.do('pe', mm, reads=[b_wdn] + b_aT, writes=[y2bk[yb]])
                p.do('dve', lambda e, z=z, j=j: e.scalar_tensor_tensor(out=r2T[:, j, 0:w_], in0=z[:, 0:w_], scalar=modL(5, j), in1=r2T[:, j, 0:w_],
                                                                     op0=ALU.mult, op1=ALU.add), reads=[y2bk[yb], b_r2[j], b_dcol], writes=[b_r2[j]])
                r_i = r2i[0] % 2
                r2i[0] += 1
                p.do('act', lambda e, j=j, r_i=r_i: e.activation(out=rbb2[r_i][:, 0:w_], in_=r2T[:, j, 0:w_], func=AF.Copy), reads=[b_r2[j]], writes=[b_rbb2[r_i]])
                p.do('act', lambda e, j=j, r_i=r_i: e.activation(out=rsq2[r_i][:, 0:w_], in_=r2T[:, j, 0:w_], func=AF.Square), reads=[b_r2[j]], writes=[b_rsq2[r_i]])
                pending.append((j, r_i))
                if len(pending) > 1:
                    jj, rr = pending.pop(0)
                    stat_mm(jj, rr, jj == 0, False)
            while pending:
                jj, rr = pending.pop(0)
                stat_mm(jj, rr, jj == 0, jj == 7)
            sl = rc % 2
            p.do('act', lambda e: e.activation(out=mean2[sl][:, 0:w_], in_=zsum[:, 0:w_], func=AF.Copy, scale=1.0 / 1024), reads=[st2b[0]], writes=[b_mean2[sl]])
            p.do('dve', lambda e: e.tensor_tensor(out=m22[sl][:, 0:w_], in0=mean2[sl][:, 0:w_], in1=mean2[sl][:, 0:w_], op=ALU.mult), reads=[b_mean2[sl]], writes=[b_m22[sl]])
            p.do('dve', lambda e: e.scalar_tensor_tensor(out=rst2[sl][:, 0:w_], in0=zsq[:, 0:w_], scalar=1.0 / 1024, in1=m22[sl][:, 0:w_], op0=ALU.mult, op1=ALU.subtract),
                 reads=[st2b[1], b_m22[sl]], writes=[b_rst2[sl]])
            p.do('act', lambda e: e.activation(out=rst2[sl][:, 0:w_], in_=rst2[sl][:, 0:w_], func=AF.Sqrt, bias=eps_c, scale=1.0), reads=[b_rst2[sl]], writes=[b_rst2[sl]], extra=tconst)
            p.do('dve', lambda e: e.reciprocal(out=rst2[sl][:, 0:w_], in_=rst2[sl][:, 0:w_]), reads=[b_rst2[sl]], writes=[b_rst2[sl]])
            if mid is not None:
                mid(0)
            for j in range(8):
                p.do('dve', lambda e, j=j: e.tensor_tensor(out=r2T[:, j, 0:w_], in0=r2T[:, j, 0:w_], in1=mean2[sl][:, 0:w_], op=ALU.subtract),
                     reads=[b_r2[j], b_mean2[sl]], writes=[b_r2[j]])
                p.do('dve', lambda e, j=j: e.tensor_tensor(out=r2T[:, j, 0:w_], in0=r2T[:, j, 0:w_], in1=rst2[sl][:, 0:w_], op=ALU.mult),
                     reads=[b_r2[j], b_rst2[sl]], writes=[b_r2[j]])
                p.do('act', lambda e, j=j: e.activation(out=r2T[:, j, 0:w_], in_=r2T[:, j, 0:w_], func=AF.Identity, scale=pv('ln2_g', j), bias=pv('ln2_b', j)),
                     reads=[b_r2[j], b_pvec], writes=[b_r2[j]])
            if mid is not None:
                mid(1)

        def out_transposes(rc):
            s_, e_ = ranges[rc]
            w_ = e_ - s_
            for it_, i0 in enumerate(range(0, w_, 128)):
                wi = min(128, w_ - i0)
                T = bank(4, 2) if it_ % 2 == 0 else bank(6, 2)
                tb0, tb1 = (y2bk[0], y2bk[1]) if it_ % 2 == 0 else (st2b[0], st2b[1])

                def mm(e, i0=i0, wi=wi, T=T):
                    ins = None
                    for j in range(8):
                        ins = e.transpose(out=T[0:wi, j * 128:(j + 1) * 128], in_=r2T[:, j, i0:i0 + wi], identity=identF)
                    return ins
                tb0.begin()
                tb1.begin()
                w = _merge(_merge({}, tb0.prev), tb1.prev)
                tok = p.do('pe', mm, reads=b_r2, extra=[w, tconst])
                _merge(tb0.wr, tok)
                _merge(tb1.wr, tok)
                o_ = oi[0] % 2
                oi[0] += 1
                tk = p.do('act', lambda e, o_=o_, wi=wi, T=T: e.activation(out=ost[o_][0:wi, :], in_=T[0:wi, :], func=AF.Copy),
                          reads=[tb0, tb1], writes=[b_ost[o_]])
                p.dma('sp', 'ost%d' % o_, out_d[s_ + i0:s_ + i0 + wi, :], ost[o_][0:wi, :], reads=[b_ost[o_]])

        NPRE = 6
        state = {}
        xt_prefetch(0)
        up_pairs(0, list(range(4)), state)
        p.dma('sp', 'wdnld', wdn_bf.rearrange("p m c -> p (m c)"), wdns_d, reads=[b_scr], writes=[b_wdn])
        up_pairs(0, list(range(4, MFF)), state)
        for rc in range(nchk):
            if rc + 1 < nchk:
                down_and_ln(rc, mid=lambda part, rc=rc: up_pairs(rc + 1, [0, 1] if part == 0 else list(range(2, NPRE)), state))
            else:
                down_and_ln(rc)
            out_transposes(rc)
            if rc + 1 < nchk:
                up_pairs(rc + 1, list(range(NPRE, 14)), state)
                xt_prefetch(rc + 1)
                up_pairs(rc + 1, list(range(14, MFF)), state)
        finish()
    return nc


def kernel(**inputs):
    inp = {k: np.asarray(v) for k, v in inputs.items()}
    B, N, _ = inp['x'].shape
    NC = inp['ctx'].shape[1]
    nc = build_program(N=N, NC=NC)
    cosT, sinT = rope_tables(N)
    ident = np.eye(128, dtype=np.float32)
    shared = {
        'ident': ident, 'cosT': cosT, 'sinT': sinT,
        'w_ada': np.ascontiguousarray(inp['w_ada'][0], dtype=np.float32),
        'w_in': np.ascontiguousarray(inp['w_in'][0], dtype=np.float32),
        'w_uq': np.ascontiguousarray(inp['w_uq'][0], dtype=np.float32),
        'w_ukv': np.ascontiguousarray(inp['w_ukv'][0], dtype=np.float32),
        'w_o': np.ascontiguousarray(inp['w_o'][0], dtype=np.float32),
        'w_up': np.ascontiguousarray(inp['w_up'][0], dtype=np.float32),
        'w_down': np.ascontiguousarray(inp['w_down'][0], dtype=np.float32),
    }
    in_maps = []
    for b in range(B):
        m = dict(shared)
        m['x'] = np.ascontiguousarray(inp['x'][b], dtype=np.float32)
        m['ctx'] = np.ascontiguousarray(inp['ctx'][b], dtype=np.float32)
        m['pvec'] = pack_pvec(inp, b)
        in_maps.append(m)
    res = run_bass_kernel_spmd(nc, in_maps, core_ids=list(range(B)))
    return np.stack([np.asarray(r['out'], dtype=np.float32) for r in res.results], axis=0)
```

```python
import numpy as np
import ml_dtypes
from contextlib import ExitStack
import concourse.bass as bass
import concourse.mybir as mybir
from concourse.bass_utils import run_bass_kernel_spmd

F32 = mybir.dt.float32
BF16 = mybir.dt.bfloat16
U8 = mybir.dt.uint8
AF = mybir.ActivationFunctionType
ALU = mybir.AluOpType

D = 1024
KD = 8
DIN = 1696
DINX = 1728
NH = 8
DFF = 2816
MUP = 44
MFF = 22
ALPHA = 2.0 ** 0.25
EPS = 1e-5
GRID_W = 64
SCALE = 96.0 ** -0.5

PV = {}
_o = 0
for _n, _c in [('cs', 16), ('b_ada', 48), ('conv_b', 4), ('conv_g', 4), ('conv_lb', 4), ('conv_w', 124),
               ('qg', 3), ('kvg', 2), ('b_o', 8), ('ln1_g', 8), ('ln1_b', 8), ('ffn_w', 132), ('ffn_b', 44),
               ('b_down', 8), ('ln2_g', 8), ('ln2_b', 8)]:
    PV[_n] = (_o, _c)
    _o += _c
NV = _o


def colT(v, nchunk):
    return np.ascontiguousarray(np.asarray(v, np.float32).reshape(nchunk, 128).T)


def pack_pvec(inp, b):
    pv = np.zeros((128, NV), np.float32)

    def put(name, arr):
        o, c = PV[name]
        assert arr.shape == (128, c), (name, arr.shape, c)
        pv[:, o:o + c] = arr

    cs = np.zeros((128, 16), np.float32)
    cs[:, 0::2] = colT(inp['c'][b], 8)
    cs[:, 1::2] = colT(inp['c_ctx'], 8)
    put('cs', cs)
    put('b_ada', colT(inp['b_ada'][0], 48))
    put('conv_b', colT(inp['conv_dw_b'][0], 4))
    put('conv_g', colT(inp['conv_ln_g'][0], 4))
    put('conv_lb', colT(inp['conv_ln_b'][0], 4))
    cw = np.asarray(inp['conv_dw_w'][0], np.float32)
    put('conv_w', np.ascontiguousarray(cw.reshape(31, 4, 128).transpose(2, 1, 0).reshape(128, 124)))
    put('qg', colT(inp['q_norm_g'][0], 3))
    put('kvg', colT(inp['kv_norm_g'][0], 2))
    put('b_o', colT(inp['b_o'][0], 8))
    put('ln1_g', colT(inp['ln1_g'][0], 8))
    put('ln1_b', colT(inp['ln1_b'][0], 8))
    fw = np.asarray(inp['ffn_dw_w'][0], np.float32)
    put('ffn_w', np.ascontiguousarray(fw.reshape(3, MUP, 128).transpose(2, 1, 0).reshape(128, 132)))
    put('ffn_b', colT(inp['ffn_dw_b'][0], MUP))
    put('b_down', colT(inp['b_down'][0], 8))
    put('ln2_g', colT(inp['ln2_g'][0], 8))
    put('ln2_b', colT(inp['ln2_b'][0], 8))
    return pv


def rope_tables(n):
    rows = n // GRID_W
    row = np.repeat(np.arange(rows), GRID_W).astype(np.float32)
    col = np.tile(np.arange(GRID_W), rows).astype(np.float32)
    n_freq = 8
    inv = (np.float32(10000.0) ** (-np.arange(n_freq, dtype=np.float32) / np.float32(n_freq))).astype(np.float32)
    ang = np.concatenate([row[:, None] * inv, col[:, None] * inv], axis=-1).astype(np.float32)
    cos = np.cos(ang).astype(np.float32).T
    sin = np.sin(ang).astype(np.float32).T
    cosT = np.ascontiguousarray(np.tile(cos, (6, 1)))
    sinT = np.ascontiguousarray(np.tile(sin, (6, 1)))
    return cosT, sinT


class Buf:
    __slots__ = ('wr', 'rd', 'prev')

    def __init__(self):
        self.wr = {}
        self.rd = {}
        self.prev = {}

    def begin(self):
        self.prev = _merge(_merge({}, self.wr), self.rd)
        self.wr = {}
        self.rd = {}


def _merge(d, t):
    if t is None:
        return d
    if isinstance(t, tuple) and len(t) == 2 and isinstance(t[0], str):
        k, v = t
        if d.get(k, 0) < v:
            d[k] = v
    elif isinstance(t, dict):
        for k, v in t.items():
            if d.get(k, 0) < v:
                d[k] = v
    else:
        for x in t:
            _merge(d, x)
    return d


class Prog:
    ENGS = ('pe', 'act', 'dve', 'pool', 'sp')
    DMA_LIMIT = 8

    def __init__(self, nc, es):
        self.nc = nc
        self.es = es
        self.q = {k: [] for k in self.ENGS}
        self.sem = {}
        self.cnt = {}
        self.waited = {k: {} for k in self.ENGS}
        self.dma_fifo = {}
        for k in self.ENGS:
            self.newsem(k)
        self.nsem = 0

    def newsem(self, key):
        if key not in self.sem:
            self.sem[key] = self.es.enter_context(self.nc.semaphore("s_" + str(key)))
            self.cnt[key] = 0
        return key

    def _waits(self, eng, waits):
        d = _merge({}, waits)
        for key, v in d.items():
            if self.waited[eng].get(key, 0) < v:
                self.waited[eng][key] = v
                s = self.sem[key]
                self.q[eng].append(lambda e, s=s, v=v: e.wait_ge(s, v))

    def op(self, eng, fn, waits=(), inc=True):
        self._waits(eng, waits)
        if inc:
            self.cnt[eng] += 1
            s = self.sem[eng]
            self.q[eng].append(lambda e, fn=fn, s=s: fn(e).then_inc(s, 1))
            return (eng, self.cnt[eng])
        self.q[eng].append(lambda e, fn=fn: fn(e))
        return None

    def do(self, eng, fn, reads=(), writes=(), pwrites=(), extra=()):
        w = _merge({}, extra)
        for b in reads:
            _merge(w, b.wr)
        for b in writes:
            b.begin()
            _merge(w, b.prev)
        for b in pwrites:
            _merge(w, b.prev)
        tok = self.op(eng, fn, w, inc=True)
        for b in reads:
            _merge(b.rd, tok)
        for b in writes:
            _merge(b.wr, tok)
        for b in pwrites:
            _merge(b.wr, tok)
        return tok

    def dma(self, eng, semkey, out, in_, reads=(), writes=(), pwrites=(), extra=(), **kw):
        self.newsem(semkey)
        w = _merge({}, extra)
        for b in reads:
            _merge(w, b.wr)
        for b in writes:
            b.begin()
            _merge(w, b.prev)
        for b in pwrites:
            _merge(w, b.prev)
        self._waits(eng, w)
        fifo = self.dma_fifo.setdefault(eng, [])
        while len(fifo) >= self.DMA_LIMIT:
            k_old = fifo.pop(0)
            self._waits(eng, (k_old, self.cnt[k_old]))
        fifo.append(semkey)
        self.cnt[semkey] += 16
        s = self.sem[semkey]
        self.q[eng].append(lambda e, s=s, out=out, in_=in_, kw=kw: e.dma_start(out=out, in_=in_, **kw).then_inc(s, 16))
        tok = (semkey, self.cnt[semkey])
        for b in reads:
            _merge(b.rd, tok)
        for b in writes:
            _merge(b.wr, tok)
        for b in pwrites:
            _merge(b.wr, tok)
        return tok

    def barrier(self, extra=()):
        toks = _merge({}, extra)
        for k in self.ENGS:
            if k == 'sp':
                continue
            if self.cnt[k] > 0:
                _merge(toks, (k, self.cnt[k]))
        for k, v in self.cnt.items():
            if k not in self.ENGS and v > 0:
                _merge(toks, (k, v))
        for k in self.ENGS:
            self._waits(k, toks)
        return toks

    def run(self):
        with self.nc.Block() as block:
            @block.tensor
            def _(e):
                for f in self.q['pe']:
                    f(e)

            @block.scalar
            def _(e):
                for f in self.q['act']:
                    f(e)

            @block.vector
            def _(e):
                for f in self.q['dve']:
                    f(e)

            @block.gpsimd
            def _(e):
                for f in self.q['pool']:
                    f(e)

            @block.sync
            def _(e):
                for f in self.q['sp']:
                    f(e)


class Arena:
    def __init__(self, ap, size):
        self.ap = ap
        self.size = size
        self.top = 0
        self.marks = []

    def alloc(self, nbytes, dt, shape=None):
        off = self.top
        self.top += (nbytes + 31) // 32 * 32
        assert self.top <= self.size, ("arena overflow", self.top, self.size)
        v = self.ap[:, off:off + nbytes].bitcast(dt)
        return v

    def raw(self, nbytes):
        off = self.top
        self.top += (nbytes + 31) // 32 * 32
        assert self.top <= self.size, ("arena overflow", self.top, self.size)
        return self.ap[:, off:off + nbytes]

    def mark(self):
        self.marks.append(self.top)

    def release(self):
        self.top = self.marks.pop()


def build_program(N=4096, NC=256, upto='all', dbg=()):
    assert N % 512 == 0 and NC % 256 == 0
    PADC = N + 32
    NK = N + NC
    NKT = NK // 128
    NCH = N // 512
    nc = bass.Bass("TRN2", target_bir_lowering=False)
    dr = {}

    def din(name, shape, dt=F32):
        dr[name] = nc.dram_tensor(name, shape, dt, kind="ExternalInput").ap()
        return dr[name]

    if upto != 'S':
        x_d = din("x", [N, D])
        ctx_d = din("ctx", [NC, D])
        cos_d = din("cosT", [96, N])
        sin_d = din("sinT", [96, N])
    pvec_d = din("pvec", [128, NV])
    ident_d = din("ident", [128, 128])
    wada_d = din("w_ada", [D, 6 * D])
    win_d = din("w_in", [D, DIN])
    if upto not in ('S', 'A', 'B'):
        wuq_d = din("w_uq", [384, 768])
        wukv_d = din("w_ukv", [256, 1024])
    if upto not in ('S', 'A', 'B', 'C1'):
        wo_d = din("w_o", [D, D])
        wup_d = din("w_up", [D, 2 * DFF])
        wdn_d = din("w_down", [DFF, D])
    out_d = nc.dram_tensor("out", [N, D], F32, kind="ExternalOutput").ap()
    xts_d = nc.dram_tensor("xts", [8, 128, N], F32).ap()
    wups_d = nc.dram_tensor("wups", [11, 128, 8 * 512], BF16).ap()
    wdns_d = nc.dram_tensor("wdns", [128, MFF * 1024], BF16).ap()
    dbg_d = {}
    for name, shape in dbg:
        dbg_d[name] = nc.dram_tensor(name, shape, F32, kind="ExternalOutput").ap()

    with ExitStack() as es:
        p = Prog(nc, es)
        ARENA_BYTES = 206 * 1024
        arena_t = es.enter_context(nc.sbuf_tensor("arena", [128, ARENA_BYTES], U8))
        ps_t = es.enter_context(nc.psum_tensor("ps", [128, 8, 512], F32))
        A = Arena(arena_t, ARENA_BYTES)

        def bank(i, n=1):
            if n == 1:
                return ps_t[:, i, :]
            return ps_t[:, i:i + n, :].rearrange("p a b -> p (a b)")

        pvec = A.alloc(NV * 4, F32)
        dcol = A.alloc(160 * 4, F32)
        identF = A.alloc(512, F32)
        identB = A.alloc(256, BF16)
        onesB = A.alloc(256, BF16)
        onesF = A.alloc(512, F32)
        kcol = A.alloc(32, F32)
        stats = A.alloc(40 * 2 * 4, F32).rearrange("p (t s) -> p t s", s=2)
        off_dead0 = A.top
        cqn = A.alloc(3 * N * 2, BF16).rearrange("p (m t) -> p m t", m=3)
        ckvn = A.alloc(2 * NK * 2, BF16).rearrange("p (m t) -> p m t", m=2)
        KT = A.alloc(2 * NK * 2, BF16).rearrange("p (s t) -> p s t", s=2)
        off_dead1 = A.top
        cat = A.alloc(8 * PADC * 2, BF16).rearrange("p (c t) -> p c t", c=8)
        b_pvec, b_dcol, b_const, b_stats = Buf(), Buf(), Buf(), Buf()
        b_cqn = [Buf() for _ in range(NCH)]
        b_ckvn = [Buf() for _ in range(NCH + 1)]
        b_krope = [Buf() for _ in range(NCH + 1)]

        def pv(name, j=None, n=1):
            o, c = PV[name]
            if j is None:
                return pvec[:, o:o + c]
            return pvec[:, o + j:o + j + n]

        DC = {}
        _off = [0]

        def dc_alloc(name, n):
            DC[name] = (_off[0], n)
            _off[0] += n
            assert _off[0] <= 160

        for nm, n in [('modL', 48), ('modC', 48), ('sc1p', 8), ('sc1pc', 8), ('g1bo', 8), ('sc2p', 8), ('G2', 8),
                      ('B2', 8), ('A3', 8), ('B3', 8)]:
            dc_alloc(nm, n)

        def dcv(name, j=None, n=1):
            o, c = DC[name]
            if j is None:
                return dcol[:, o:o + c]
            return dcol[:, o + j:o + j + n]

        def modL(s, j=None):
            o = DC['modL'][0] + s * 8
            return dcol[:, o:o + 8] if j is None else dcol[:, o + j:o + j + 1]

        def modC(s, j=None):
            o = DC['modC'][0] + s * 8
            return dcol[:, o:o + 8] if j is None else dcol[:, o + j:o + j + 1]

        p.dma('sp', 'ld0', pvec, pvec_d, writes=[b_pvec])
        p.dma('sp', 'ld1', identF, ident_d, writes=[b_const])
        p.do('pool', lambda e: e.memset(dcol, 0.0), writes=[b_dcol])
        p.do('pool', lambda e: e.memset(stats.rearrange("p t s -> p (t s)"), 0.0), writes=[b_stats])
        t1 = p.do('pool', lambda e: e.tensor_copy(out=identB, in_=identF), reads=[b_const])
        t2 = p.do('pool', lambda e: e.memset(onesB, 1.0))
        t3 = p.do('pool', lambda e: e.memset(onesF, 1.0))
        t4 = p.do('pool', lambda e: e.memset(kcol[:, 0:1], EPS))
        t5 = p.do('pool', lambda e: e.memset(kcol[:, 1:2], 0.0))
        tconst = _merge({}, [t1, t2, t3, t4, t5])
        eps_c = kcol[:, 0:1]
        import os as _os
        _stop = _os.environ.get('STOP', '')

        class _Early(Exception):
            pass

        def early(tag):
            if _stop == tag:
                p.barrier()
                p.dma('sp', 'dbg2', dbg_d['d_dcol'], dcol)
                p._waits('sp', ('dbg2', p.cnt['dbg2']))
                p.run()
                raise _Early(nc)

        A.mark()
        w_in_bf = A.alloc(8 * DINX * 2, BF16).rearrange("p (k c) -> p k c", k=8)
        b_win = Buf()
        A.mark()
        wada = [A.alloc(6 * D * 4, F32) for _ in range(2)]
        b_wada = [Buf(), Buf()]
        cs2 = A.alloc(16 * 4, F32)
        b_cs2 = Buf()
        early('s0')
        p.do('act', lambda e: e.activation(out=cs2, in_=pv('cs'), func=AF.Silu), reads=[b_pvec], writes=[b_cs2])
        early('s1')
        wqs = [wada[0][:, 0:3072], wada[0][:, 3072:6144], wada[1][:, 0:3072], wada[1][:, 3072:6144]]
        b_wqs = [Buf() for _ in range(4)]
        rowt = [A.alloc(2048, F32) for _ in range(2)]
        b_rowt = [Buf(), Buf()]
        accb = [Buf() for _ in range(6)]
        b_T = Buf()
        b_T.begin()
        mpsT = bank(6)[:, 0:96]
        qi = 0
        win_k = [0]
        b_win.begin()
        win_st = wqs[3][:, 0:DIN]

        def win_piece():
            k = win_k[0]
            if k >= 8:
                return
            win_k[0] += 1
            p.dma('sp', 'wq3', win_st, win_d[k * 128:(k + 1) * 128, :], writes=[b_wqs[3]])
            p.do('dve', lambda e: e.tensor_copy(out=w_in_bf[:, k, 0:DIN], in_=win_st), reads=[b_wqs[3]], pwrites=[b_win])
            p.do('dve', lambda e: e.tensor_scalar(out=w_in_bf[:, k, DIN:DIN + 16], in0=win_st[:, 1680:1696], scalar1=-1.0, scalar2=None, op0=ALU.mult),
                 reads=[b_wqs[3]], pwrites=[b_win])
            p.do('dve', lambda e: e.tensor_copy(out=w_in_bf[:, k, DIN + 16:DIN + 32], in_=win_st[:, 1664:1680]), reads=[b_wqs[3]], pwrites=[b_win])

        for r in range(2):
            for k in range(8):
                sl = qi % 3
                qi += 1
                p.dma('sp', 'wq%d' % sl, wqs[sl], wada_d[k * 128:(k + 1) * 128, r * 3072:(r + 1) * 3072], writes=[b_wqs[sl]])
                if qi % 2 == 0:
                    win_piece()

                def mm(e, sl=sl, k=k):
                    ins = None
                    for g in range(6):
                        ins = e.matmul(bank(g)[0:2, :], lhsT=cs2[:, 2 * k:2 * k + 2], rhs=wqs[sl][:, g * 512:(g + 1) * 512], start=(k == 0), stop=(k == 7))
                    return ins
                if k == 0:
                    p.do('pe', mm, reads=[b_wqs[sl], b_cs2], writes=accb)
                else:
                    p.do('pe', mm, reads=[b_wqs[sl], b_cs2], pwrites=accb)
            for g in range(6):
                ri = (r * 6 + g) % 2
                if g % 2 == 0:
                    p.do('act', lambda e, ri=ri, g=g: e.activation(out=rowt[ri][0:2, :], in_=bank(g)[0:2, :], func=AF.Copy), reads=[accb[g]], writes=[b_rowt[ri]])
                else:
                    p.do('dve', lambda e, ri=ri, g=g: e.tensor_copy(out=rowt[ri][0:2, :], in_=bank(g)[0:2, :]), reads=[accb[g]], writes=[b_rowt[ri]])

                def tr(e, ri=ri, r=r, g=g):
                    ins = None
                    for q_ in range(4):
                        f = r * 24 + g * 4 + q_
                        ins = e.matmul(mpsT[:, 2 * f:2 * f + 2], lhsT=rowt[ri][0:2, q_ * 128:(q_ + 1) * 128], rhs=identF[0:2, 0:2], start=True, stop=True)
                    return ins
                p.do('pe', tr, reads=[b_rowt[ri]], pwrites=[b_T], extra=tconst)
        early('s2')
        acc3 = mpsT.rearrange("p (f t) -> p f t", t=2)
        p.do('dve', lambda e: e.tensor_tensor(out=dcv('modL'), in0=acc3[:, :, 0], in1=pv('b_ada'), op=ALU.add),
             reads=[b_T, b_pvec], writes=[b_dcol])
        p.do('dve', lambda e: e.tensor_tensor(out=dcv('modC'), in0=acc3[:, :, 1], in1=pv('b_ada'), op=ALU.add),
             reads=[b_T, b_pvec, b_dcol], writes=[b_dcol])
        early('s3')

        def dve_small(fn):
            p.do('dve', fn, reads=[b_dcol, b_pvec], writes=[b_dcol])

        dve_small(lambda e: e.tensor_scalar(out=dcv('sc1p'), in0=modL(1), scalar1=1.0, scalar2=None, op0=ALU.add))
        dve_small(lambda e: e.tensor_scalar(out=dcv('sc1pc'), in0=modC(1), scalar1=1.0, scalar2=None, op0=ALU.add))
        dve_small(lambda e: e.tensor_tensor(out=dcv('g1bo'), in0=modL(2), in1=pv('b_o'), op=ALU.mult))
        dve_small(lambda e: e.tensor_scalar(out=dcv('sc2p'), in0=modL(4), scalar1=1.0, scalar2=None, op0=ALU.add))
        dve_small(lambda e: e.tensor_tensor(out=dcv('G2'), in0=pv('ln1_g'), in1=dcv('sc2p'), op=ALU.mult))
        dve_small(lambda e: e.tensor_tensor(out=dcv('B2'), in0=pv('ln1_b'), in1=dcv('sc2p'), op=ALU.mult))
        dve_small(lambda e: e.tensor_tensor(out=dcv('B2'), in0=dcv('B2'), in1=modL(3), op=ALU.add))
        dve_small(lambda e: e.tensor_scalar(out=dcv('A3'), in0=pv('ln1_g'), scalar1=ALPHA, scalar2=None, op0=ALU.mult))
        dve_small(lambda e: e.tensor_tensor(out=dcv('B3'), in0=modL(5), in1=pv('b_down'), op=ALU.mult))
        dve_small(lambda e: e.scalar_tensor_tensor(out=dcv('B3'), in0=pv('ln1_b'), scalar=ALPHA, in1=dcv('B3'),
                                                  op0=ALU.mult, op1=ALU.add))

        early('s4')
        while win_k[0] < 8:
            win_piece()
        p.barrier()
        A.release()
        if upto == 'S':
            p.dma('sp', 'dbg2', dbg_d['d_dcol'], dcol)
            p._waits('sp', ('dbg2', p.cnt['dbg2']))
            p.run()
            return nc

        NXT = 4
        xt = [A.alloc(D * 4, F32) for _ in range(NXT)]
        b_xt = [Buf() for _ in range(NXT)]
        cqf = [A.alloc(512 * 4, F32) for _ in range(5)]
        b_cqf = [Buf() for _ in range(5)]
        sqb = [A.alloc(512 * 2, BF16) for _ in range(5)]
        b_sqb = [Buf() for _ in range(5)]
        tmpf = [A.alloc(512 * 4, F32) for _ in range(6)]
        b_tmpf = [Buf() for _ in range(6)]
        cst = [A.alloc(2 * 512 * 4, F32).rearrange("p (s t) -> p s t", s=2) for _ in range(2)]
        b_cst = [Buf(), Buf()]
        st6 = A.alloc(16 * 4 * 4, F32).rearrange("p (s c) -> p s c", s=4)
        b_st6 = [Buf() for _ in range(4)]
        if 4 * PADC >= 12288:
            catflat = cat[:, 0:4, :].rearrange("p c t -> p (c t)")
        else:
            catflat = A.alloc(12288 * 2, BF16)
        hT = [catflat[:, s * 4096:(s + 1) * 4096].rearrange("p (k t) -> p k t", k=8) for s in range(2)]
        xnb = [catflat[:, 8192 + s * 1024: 8192 + (s + 1) * 1024] for s in range(4)]
        b_hT = [[Buf() for _ in range(8)] for _ in range(2)]
        b_xnb = [Buf() for _ in range(4)]
        vT = cat[:, 4:8, :]
        b_vT = [Buf() for _ in range(NCH)]
        b_vpad = Buf()
        p.do('pool', lambda e: e.memset(vT[:, :, 0:15], 0.0), writes=[b_vpad])
        p.do('pool', lambda e: e.memset(vT[:, :, 15 + N:30 + N], 0.0), reads=[b_vpad], pwrites=[b_vpad])
        early('p0')
        tp_ps = ps_t[:, 0:2, :].rearrange("p a b -> p (a b)").bitcast(BF16).rearrange("p (j t) -> p j t", j=8)
        b_tp = Buf()
        zb = [Buf() for _ in range(6)]
        zring = [0]

        def znext():
            i = zring[0] % 6
            zring[0] += 1
            return bank(2 + i), zb[i]

        chunks = [('lat', c * 512, 512, c) for c in range(NCH)] + [('ctx', 0, NC, NCH)]
        xcount = [0]
        tilecount = [0]

        xslots = {}

        def xload(ci):
            kind, t0, nt, cidx = chunks[ci]
            src = x_d if kind == 'lat' else ctx_d
            xslots[ci] = []
            for i in range(nt // 128):
                xs = xcount[0] % NXT
                xcount[0] += 1
                xslots[ci].append(xs)
                p.dma('sp', 'xt%d' % xs, xt[xs], src[t0 + i * 128:t0 + (i + 1) * 128, :], writes=[b_xt[xs]])

        def pro_stats(ci, i):
            xs = xslots[ci][i]
            p.do('dve', lambda e: e.bn_stats(out=st6[:, i, 0:6], in_=xt[xs][:, 0:512]), reads=[b_xt[xs]], writes=[b_st6[i]])
            p.do('dve', lambda e: e.bn_stats(out=st6[:, i, 6:12], in_=xt[xs][:, 512:1024]), reads=[b_xt[xs]], pwrites=[b_st6[i]])
            p.do('dve', lambda e: e.bn_aggr(out=st6[:, i, 12:14], in_=st6[:, i, 0:12]), reads=[b_st6[i]], pwrites=[b_st6[i]])

        def pro_norm(ci):
            kind, t0, nt, cidx = chunks[ci]
            ntile = nt // 128
            gt0 = tile_base[ci]
            bs = [b_st6[i] for i in range(ntile)]
            p.do('act', lambda e: e.activation(out=st6[:, 0:ntile, 14:15], in_=st6[:, 0:ntile, 13:14], func=AF.Sqrt, bias=eps_c, scale=1.0),
                 reads=bs, pwrites=bs, extra=tconst)
            p.do('dve', lambda e: e.reciprocal(out=stats[:, gt0:gt0 + ntile, 0:1], in_=st6[:, 0:ntile, 14:15]), reads=bs, pwrites=[b_stats])
            p.do('dve', lambda e: e.scalar_tensor_tensor(out=stats[:, gt0:gt0 + ntile, 1:2], in0=st6[:, 0:ntile, 12:13], scalar=-1.0,
                                                         in1=stats[:, gt0:gt0 + ntile, 0:1], op0=ALU.mult, op1=ALU.mult),
                 reads=bs + [b_stats], pwrites=[b_stats])
            for i in range(ntile):
                xs = xslots[ci][i]
                gt = gt0 + i
                p.do('act', lambda e, i=i, xs=xs, gt=gt: e.activation(out=xnb[i], in_=xt[xs], func=AF.Identity, scale=stats[:, gt, 0:1], bias=stats[:, gt, 1:2]),
                     reads=[b_xt[xs], b_stats], writes=[b_xnb[i]])

        def pro_transposes(ci):
            kind, t0, nt, cidx = chunks[ci]
            sl = ci % 2
            scol = 'sc1p' if kind == 'lat' else 'sc1pc'
            for b_ in b_hT[sl]:
                b_.begin()
            for i in range(nt // 128):
                def tr(e, i=i):
                    ins = None
                    for j in range(8):
                        ins = e.transpose(out=tp_ps[:, j, (i % 2) * 128:(i % 2) * 128 + 128],
                                          in_=xnb[i][:, j * 128:(j + 1) * 128], identity=identB)
                    return ins
                if i % 2 == 0:
                    p.do('pe', tr, reads=[b_xnb[i]], writes=[b_tp], extra=tconst)
                else:
                    p.do('pe', tr, reads=[b_xnb[i]], pwrites=[b_tp], extra=tconst)
                if i % 2 == 1:
                    pair = i // 2
                    for j in range(8):
                        o = hT[sl][:, j, pair * 256:(pair + 1) * 256]
                        if j < 4:
                            p.do('act', lambda e, o=o, j=j: e.activation(out=o, in_=tp_ps[:, j, :], func=AF.Identity,
                                                                         scale=dcv(scol, j), bias=(modL(0, j) if kind == 'lat' else modC(0, j))),
                                 reads=[b_tp, b_dcol], pwrites=[b_hT[sl][j]])
                        else:
                            p.do('dve', lambda e, o=o, j=j: e.tensor_scalar(out=o, in0=tp_ps[:, j, :], scalar1=dcv(scol, j),
                                                                            scalar2=(modL(0, j) if kind == 'lat' else modC(0, j)),
                                                                            op0=ALU.mult, op1=ALU.add),
                                 reads=[b_tp, b_dcol], pwrites=[b_hT[sl][j]])

        def zgroup(sl, nt, c0, m):
            z, zbuf = znext()

            def mm(e, z=z):
                ins = None
                for k in range(8):
                    ins = e.matmul(z[0:m, 0:nt], lhsT=w_in_bf[:, k, c0:c0 + m], rhs=hT[sl][:, k, 0:nt],
                                   start=(k == 0), stop=(k == 7))
                return ins
            p.do('pe', mm, reads=b_hT[sl] + [b_win], writes=[zbuf])
            return z, zbuf

        tmpi = [0]

        def tnext():
            i = tmpi[0] % 6
            tmpi[0] += 1
            return tmpf[i], b_tmpf[i]

        def rms_front(sl, nt, c0, nm, slot0):
            for m in range(nm):
                z, zbuf = zgroup(sl, nt, c0 + m * 128, 128)
                f, bf_ = cqf[slot0 + m], b_cqf[slot0 + m]
                p.do('act', lambda e, z=z, f=f: e.activation(out=f[:, 0:nt], in_=z[:, 0:nt], func=AF.Copy),
                     reads=[zbuf], writes=[bf_])
                p.do('pool', lambda e, f=f, m=m: e.tensor_tensor(out=sqb[slot0 + m][:, 0:nt], in0=f[:, 0:nt], in1=f[:, 0:nt], op=ALU.mult),
                     reads=[bf_], writes=[b_sqb[slot0 + m]])

        def rms_back(nt, nm, slot0, dim, dst, dstbuf, tcol0):
            z, zbuf = znext()

            def mm(e, z=z):
                ins = None
                for m in range(nm):
                    ins = e.matmul(z[:, 0:nt], lhsT=onesB, rhs=sqb[slot0 + m][:, 0:nt], start=(m == 0), stop=(m == nm - 1))
                return ins
            p.do('pe', mm, reads=[b_sqb[slot0 + m] for m in range(nm)], writes=[zbuf], extra=tconst)
            r, rb = tnext()
            p.do('act', lambda e, z=z, r=r: e.activation(out=r[:, 0:nt], in_=z[:, 0:nt], func=AF.Sqrt, bias=eps_c, scale=1.0 / dim),
                 reads=[zbuf], writes=[rb], extra=tconst)
            p.do('dve', lambda e, r=r: e.reciprocal(out=r[:, 0:nt], in_=r[:, 0:nt]), reads=[rb], writes=[rb])
            for m in range(nm):
                f, bf_ = cqf[slot0 + m], b_cqf[slot0 + m]
                p.do('pool' if m % 2 == 0 else 'dve',
                     lambda e, f=f, r=r, m=m: e.tensor_tensor(out=dst[:, m, tcol0:tcol0 + nt], in0=f[:, 0:nt], in1=r[:, 0:nt], op=ALU.mult),
                     reads=[bf_, rb], pwrites=[dstbuf])

        def groups_glu(ci):
            kind, t0, nt, cidx = chunks[ci]
            sl = ci % 2
            if kind != 'lat':
                return
            cs_ = ci % 2
            p.dma('sp', 'cst%d' % cs_, cst[cs_][0:32, 0, :], cos_d[0:32, t0:t0 + 512], writes=[b_cst[cs_]])
            p.dma('sp', 'cst%d' % cs_, cst[cs_][0:32, 1, :], sin_d[0:32, t0:t0 + 512], pwrites=[b_cst[cs_]])
            b_vT[cidx].begin()
            for c in range(4):
                za, zab = zgroup(sl, nt, c * 128, 128)
                zg, zgb = zgroup(sl, nt, 512 + c * 128, 128)
                tsg, tsb = tnext()
                p.do('act', lambda e, zg=zg, tsg=tsg: e.activation(out=tsg, in_=zg, func=AF.Sigmoid), reads=[zgb], writes=[tsb])
                p.do('dve', lambda e, za=za, tsg=tsg, c=c: e.tensor_tensor(out=vT[:, c, 15 + t0:15 + t0 + 512], in0=za, in1=tsg, op=ALU.mult),
                     reads=[zab, tsb], pwrites=[b_vT[cidx]])
                if ci + 1 < len(chunks) and c < chunks[ci + 1][2] // 128:
                    pro_stats(ci + 1, c)

        def groups_lat(ci):
            kind, t0, nt, cidx = chunks[ci]
            sl = ci % 2
            if kind == 'lat':
                rms_front(sl, nt, 1024, 3, 0)
            rms_front(sl, nt, 1408, 2, 3)
            kcol0 = t0 if kind == 'lat' else N
            b_krope[cidx].begin()
            za, zab = zgroup(sl, nt, 1664, 32)
            if kind == 'lat':
                zb_, zbb = zgroup(sl, nt, 1696, 32)
                t1_, t1b = tnext()
                t2_, t2b = tnext()
                cs_ = ci % 2
                p.do('dve', lambda e, za=za, t1_=t1_: e.tensor_tensor(out=t1_[0:32, :], in0=za[0:32, :], in1=cst[cs_][0:32, 0, :], op=ALU.mult),
                     reads=[zab, b_cst[cs_]], writes=[t1b])
                p.do('dve', lambda e, zb_=zb_, t2_=t2_: e.tensor_tensor(out=t2_[0:32, :], in0=zb_[0:32, :], in1=cst[cs_][0:32, 1, :], op=ALU.mult),
                     reads=[zbb, b_cst[cs_]], writes=[t2b])
                p.do('pool', lambda e, t1_=t1_, t2_=t2_: e.tensor_tensor(out=KT[64:96, 0, kcol0:kcol0 + nt], in0=t1_[0:32, :], in1=t2_[0:32, :], op=ALU.add),
                     reads=[t1b, t2b], pwrites=[b_krope[cidx]])
            else:
                p.do('act', lambda e, za=za: e.activation(out=KT[64:96, 0, kcol0:kcol0 + nt], in_=za[0:32, 0:nt], func=AF.Copy),
                     reads=[zab], pwrites=[b_krope[cidx]])
            p.do('pool', lambda e: e.tensor_copy(out=KT[64:96, 1, kcol0:kcol0 + nt], in_=KT[64:96, 0, kcol0:kcol0 + nt]),
                 reads=[b_krope[cidx]], pwrites=[b_krope[cidx]])

        def groups_fin(ci):
            kind, t0, nt, cidx = chunks[ci]
            if kind == 'lat':
                b_cqn[cidx].begin()
                rms_back(nt, 3, 0, 384.0, cqn, b_cqn[cidx], t0)
            b_ckvn[cidx].begin()
            kcol0 = t0 if kind == 'lat' else N
            rms_back(nt, 2, 3, 256.0, ckvn, b_ckvn[cidx], kcol0)

        nchunks = len(chunks)
        tile_base = {}
        tb_ = 0
        for ci_ in range(nchunks):
            tile_base[ci_] = tb_
            tb_ += chunks[ci_][2] // 128
        xload(0)
        for i_ in range(chunks[0][2] // 128):
            pro_stats(0, i_)
        pro_norm(0)
        pro_transposes(0)
        if nchunks > 1:
            xload(1)
        for ci in range(nchunks):
            groups_glu(ci)
            if ci >= 1:
                groups_fin(ci - 1)
            if ci + 1 < nchunks:
                if chunks[ci][0] != 'lat':
                    for i_ in range(chunks[ci + 1][2] // 128):
                        pro_stats(ci + 1, i_)
                pro_norm(ci + 1)
            groups_lat(ci)
            if ci + 1 < nchunks:
                pro_transposes(ci + 1)
                if ci + 2 < nchunks:
                    xload(ci + 2)
        groups_fin(nchunks - 1)
        early('a5')
        tokA = p.barrier()

        def dump(name, src_ap):
            dst = dbg_d[name]
            t = p.dma('sp', 'dbg', dst, src_ap)
            return t

        def finish(extra=()):
            p._waits('sp', _merge({}, extra))
            for k, v in p.cnt.items():
                if k not in p.ENGS and v > 0:
                    p._waits('sp', (k, v))
            p.run()

        if upto == 'A':
            A.release()
            A.mark()
            dtmp = A.alloc(NK * 4, F32)
            toks = []
            last = None
            for name, src in [('d_cqn0', cqn[:, 0, :]), ('d_cqn2', cqn[:, 2, :]), ('d_ckvn0', ckvn[:, 0, :]), ('d_ckvn1', ckvn[:, 1, :]),
                              ('d_v0', vT[:, 0, 0:N + 30]), ('d_v3', vT[:, 3, 0:N + 30]), ('d_krope', KT[:, 0, :]), ('d_krope1', KT[:, 1, :])]:
                if name not in dbg_d:
                    continue
                n = src.shape[1]
                np_ = src.shape[0]
                p.op('dve', lambda e: e.memset(dtmp, 0.0), waits=[last] if last else ())
                if name.startswith('d_krope'):
                    tk = p.op('dve', lambda e, src=src, n=n: e.tensor_copy(out=dtmp[64:96, 0:n], in_=src[64:96, :]), waits=[("dve", p.cnt["dve"])])
                else:
                    tk = p.op('dve', lambda e, src=src, n=n: e.tensor_copy(out=dtmp[:, 0:n], in_=src), waits=[("dve", p.cnt["dve"])])
                last = p.dma('sp', 'dbg', dbg_d[name], dtmp[:, 0:n], extra=[tk])
                p._waits('dve', [last])
            if 'd_dcol' in dbg_d:
                last = p.dma('sp', 'dbg2', dbg_d['d_dcol'], dcol)
            if 'd_stats' in dbg_d:
                last = p.dma('sp', 'dbg3', dbg_d['d_stats'], stats.rearrange("p t s -> p (t s)"))
            finish()
            return nc
        A.release()

        def dump_cat(items):
            A.mark()
            dtmp = A.alloc(N * 4, F32)
            last = None
            for name, c, rows in items:
                if name not in dbg_d:
                    continue
                tk = p.op('dve', lambda e, c=c: e.tensor_copy(out=dtmp[:, 0:N], in_=cat[:, c, 1:N + 1]),
                          waits=[last, ("dve", p.cnt["dve"])] if last else [("dve", p.cnt["dve"])])
                last = p.dma('sp', 'dbg', dbg_d[name], dtmp[:, 0:N], extra=[tk])
            A.release()

        A.mark()
        diag = A.alloc(124 * 128 * 2, BF16).rearrange("p (i c) -> p i c", i=124)
        b_diag = Buf()
        b_diag.begin()
        for idx in range(124):
            if idx % 2 == 0:
                p.do('dve', lambda e, idx=idx: e.tensor_scalar(out=diag[:, idx, :], in0=identF, scalar1=pv('conv_w', idx), scalar2=None, op0=ALU.mult),
                     reads=[b_pvec, b_const], pwrites=[b_diag])
            else:
                p.do('act', lambda e, idx=idx: e.activation(out=diag[:, idx, :], in_=identF, func=AF.Copy, scale=pv('conv_w', idx)),
                     reads=[b_pvec, b_const], pwrites=[b_diag])
        convf = [[A.alloc(2048, F32) for _ in range(4)] for _ in range(2)]
        b_convf = [[Buf() for _ in range(4)] for _ in range(2)]
        cbb = [A.alloc(1024, BF16) for _ in range(4)]
        csq_ = [A.alloc(1024, BF16) for _ in range(4)]
        b_cbb = [Buf() for _ in range(4)]
        b_csq = [Buf() for _ in range(4)]
        meanb = [A.alloc(2048, F32) for _ in range(2)]
        m2b = [A.alloc(2048, F32) for _ in range(2)]
        rstb = [A.alloc(2048, F32) for _ in range(2)]
        b_meanb, b_m2b, b_rstb = [Buf(), Buf()], [Buf(), Buf()], [Buf(), Buf()]
        ttmp = [A.alloc(2048, F32) for _ in range(2)]
        b_ttmp = [Buf(), Buf()]
        cbk = [Buf() for _ in range(4)]
        sbk = [Buf() for _ in range(4)]
        b_cat = [[Buf() for _ in range(NCH)] for _ in range(8)]
        tt_i = [0]
        for ci in range(NCH):
            t0 = ci * 512
            sl = ci % 2
            for c in range(4):
                z = bank(c)

                def mm(e, z=z, c=c, t0=t0):
                    ins = None
                    for k in range(31):
                        ins = e.matmul(z, lhsT=diag[:, c * 31 + k, :], rhs=vT[:, c, t0 + k:t0 + k + 512], start=(k == 0), stop=(k == 30))
                    return ins
                p.do('pe', mm, reads=[b_diag, b_vpad] + b_vT, writes=[cbk[c]])
                f = convf[sl][c]
                p.do('act', lambda e, z=z, f=f, c=c: e.activation(out=f, in_=z, func=AF.Identity, bias=pv('conv_b', c), scale=1.0),
                     reads=[cbk[c], b_pvec], writes=[b_convf[sl][c]])
                p.do('pool', lambda e, f=f, c=c: e.tensor_copy(out=cbb[c], in_=f), reads=[b_convf[sl][c]], writes=[b_cbb[c]])
                p.do('dve', lambda e, f=f, c=c: e.tensor_tensor(out=csq_[c], in0=f, in1=f, op=ALU.mult), reads=[b_convf[sl][c]], writes=[b_csq[c]])
            zm, ze = bank(4 + sl), bank(6 + sl)

            def mms(e, zm=zm):
                ins = None
                for c in range(4):
                    ins = e.matmul(zm, lhsT=onesB, rhs=cbb[c], start=(c == 0), stop=(c == 3))
                return ins

            def mme(e, ze=ze):
                ins = None
                for c in range(4):
                    ins = e.matmul(ze, lhsT=onesB, rhs=csq_[c], start=(c == 0), stop=(c == 3))
                return ins
            p.do('pe', mms, reads=b_cbb, writes=[sbk[sl]], extra=tconst)
            p.do('pe', mme, reads=b_csq, writes=[sbk[2 + sl]], extra=tconst)
            p.do('act', lambda e, zm=zm, sl=sl: e.activation(out=meanb[sl], in_=zm, func=AF.Copy, scale=1.0 / 512),
                 reads=[sbk[sl]], writes=[b_meanb[sl]])
            p.do('dve', lambda e, sl=sl: e.tensor_tensor(out=m2b[sl], in0=meanb[sl], in1=meanb[sl], op=ALU.mult),
                 reads=[b_meanb[sl]], writes=[b_m2b[sl]])
            p.do('dve', lambda e, ze=ze, sl=sl: e.scalar_tensor_tensor(out=rstb[sl], in0=ze, scalar=1.0 / 512, in1=m2b[sl], op0=ALU.mult, op1=ALU.subtract),
                 reads=[sbk[2 + sl], b_m2b[sl]], writes=[b_rstb[sl]])
            p.do('act', lambda e, sl=sl: e.activation(out=rstb[sl], in_=rstb[sl], func=AF.Sqrt, bias=eps_c, scale=1.0),
                 reads=[b_rstb[sl]], writes=[b_rstb[sl]], extra=tconst)
            p.do('dve', lambda e, sl=sl: e.reciprocal(out=rstb[sl], in_=rstb[sl]), reads=[b_rstb[sl]], writes=[b_rstb[sl]])
            for c in range(4):
                ti = tt_i[0] % 2
                tt_i[0] += 1
                f = convf[sl][c]
                p.do('dve', lambda e, f=f, ti=ti, sl=sl: e.tensor_tensor(out=ttmp[ti], in0=f, in1=meanb[sl], op=ALU.subtract),
                     reads=[b_convf[sl][c], b_meanb[sl]], writes=[b_ttmp[ti]])
                p.do('dve', lambda e, ti=ti, sl=sl: e.tensor_tensor(out=ttmp[ti], in0=ttmp[ti], in1=rstb[sl], op=ALU.mult),
                     reads=[b_ttmp[ti], b_rstb[sl]], writes=[b_ttmp[ti]])
                p.do('act', lambda e, ti=ti, c=c, t0=t0: e.activation(out=cat[:, c, 1 + t0:1 + t0 + 512], in_=ttmp[ti], func=AF.Silu,
                                                                     scale=pv('conv_g', c), bias=pv('conv_lb', c)),
                     reads=[b_ttmp[ti], b_pvec], writes=[b_cat[c][ci]])
        p.barrier()
        A.release()
        if upto == 'B':
            dump_cat([('d_conv0', 0, 128), ('d_conv3', 3, 128)])
            finish()
            return nc

        A.mark()
        VT = A.alloc(2 * NKT * 128 * 2, BF16).rearrange("p (s t) -> p s t", s=2)
        w_uq_bf = A.alloc(3 * 768 * 2, BF16).rearrange("p (k c) -> p k c", k=3)
        w_uqb_bf = A.alloc(3 * 256 * 2, BF16).rearrange("p (k c) -> p k c", k=3)
        w_ukv_bf = A.alloc(2 * 1024 * 2, BF16).rearrange("p (k c) -> p k c", k=2)
        wst = [A.alloc(1024 * 4, F32) for _ in range(2)]
        b_wst = [Buf(), Buf()]
        QT = A.alloc(2 * 512 * 2, BF16).rearrange("p (s t) -> p s t", s=2)
        b_QT = [Buf(), Buf()]
        NSS = 3
        PT = A.alloc(NSS * 1024 * 2, BF16).rearrange("p (s t) -> p s t", s=NSS)
        b_PT = [Buf() for _ in range(NSS)]
        Osb = [A.alloc(2048, F32) for _ in range(2)]
        b_Osb = [Buf(), Buf()]
        csq = A.alloc(2 * 2 * 512 * 4, F32).rearrange("p (s a t) -> p s a t", s=2, a=2)
        b_csq2 = [Buf(), Buf()]
        tq = [A.alloc(2048, F32) for _ in range(4)]
        b_tq = [Buf() for _ in range(4)]
        Rt = [A.alloc(2048, F32) for _ in range(2)]
        b_Rt = [Buf(), Buf()]
        nqg = A.alloc(3 * 4, F32)
        b_w = Buf()
        b_VT = [Buf(), Buf()]
        b_KT = [Buf(), Buf()]
        b_nqg = Buf()
        p.do('dve', lambda e: e.tensor_scalar(out=nqg, in0=pv('qg'), scalar1=-1.0, scalar2=None, op0=ALU.mult), reads=[b_pvec], writes=[b_nqg])
        b_w.begin()
        for k in range(3):
            st_ = wst[k % 2][:, 0:768]
            p.dma('sp', 'wst%d' % (k % 2), st_, wuq_d[k * 128:(k + 1) * 128, :], writes=[b_wst[k % 2]])
            p.do('dve', lambda e, k=k, st_=st_: e.tensor_scalar(out=w_uq_bf[:, k, :], in0=st_, scalar1=pv('qg', k), scalar2=None, op0=ALU.mult),
                 reads=[b_wst[k % 2], b_pvec], pwrites=[b_w])
            st3 = st_.rearrange("p (h d) -> p h d", h=8)
            ob = w_uqb_bf[:, k, :].rearrange("p (h d) -> p h d", h=8)
            p.do('dve', lambda e, k=k, st3=st3, ob=ob: e.tensor_scalar(out=ob[:, :, 0:16], in0=st3[:, :, 80:96], scalar1=nqg[:, k:k + 1], scalar2=None, op0=ALU.mult),
                 reads=[b_wst[k % 2], b_nqg], pwrites=[b_w])
            p.do('dve', lambda e, k=k, st3=st3, ob=ob: e.tensor_scalar(out=ob[:, :, 16:32], in0=st3[:, :, 64:80], scalar1=pv('qg', k), scalar2=None, op0=ALU.mult),
                 reads=[b_wst[k % 2], b_pvec], pwrites=[b_w])
        for k in range(2):
            st_ = wst[(k + 1) % 2]
            p.dma('sp', 'wst%d' % ((k + 1) % 2), st_, wukv_d[k * 128:(k + 1) * 128, :], writes=[b_wst[(k + 1) % 2]])
            p.do('dve', lambda e, k=k, st_=st_: e.tensor_scalar(out=w_ukv_bf[:, k, :], in0=st_, scalar1=pv('kvg', k), scalar2=None, op0=ALU.mult),
                 reads=[b_wst[(k + 1) % 2], b_pvec], pwrites=[b_w])
        VT4 = VT.rearrange("p s (t c) -> p s t c", c=128)
        for s_ in range(2):
            b_VT[s_].begin()
            p.do('pool', lambda e, s_=s_: e.memset(VT4[:, s_, :, 64:128], 1.0), pwrites=[b_VT[s_]])
        Sb = [Buf() for _ in range(NSS)]
        Ob = [Buf()]
        Mb = [Buf()]

        Mfree = [Buf() for _ in range(7)]
        mring = {'banks': None, 'i': 0}

        def mnext():
            if mring['banks']:
                bk = mring['banks'][mring['i'] % len(mring['banks'])]
                mring['i'] += 1
                return bank(bk), (Sb[bk // 2] if False else Mfree[bk])
            return bank(7), Mb[0]

        all_ckvn = b_ckvn
        all_krope = b_krope
        KCH = [(c0, min(512, NK - c0)) for c0 in range(0, NK, 512)]

        def kv_tasks(h):
            hs = h % 2
            tasks = []

            def start():
                b_KT[hs].begin()
                b_VT[hs].begin()
            tasks.append(start)
            for (c0, n) in KCH:
                def tk(c0=c0, n=n):
                    z, zb_ = mnext()

                    def mm(e):
                        ins = None
                        for k in range(2):
                            ins = e.matmul(z[0:64, 0:n], lhsT=w_ukv_bf[:, k, 128 * h:128 * h + 64], rhs=ckvn[:, k, c0:c0 + n], start=(k == 0), stop=(k == 1))
                        return ins
                    p.do('pe', mm, reads=[b_w] + all_ckvn, writes=[zb_])
                    p.do('dve', lambda e: e.tensor_copy(out=KT[0:64, hs, c0:c0 + n], in_=z[0:64, 0:n]), reads=[zb_], pwrites=[b_KT[hs]])
                tasks.append(tk)
            for t0_ in range(0, NKT, 8):
                def tv(t0_=t0_):
                    nt_ = min(8, NKT - t0_)
                    z, zb_ = mnext()

                    def mm(e):
                        ins = None
                        for j in range(nt_):
                            kt = t0_ + j
                            for k in range(2):
                                ins = e.matmul(z[:, j * 64:(j + 1) * 64], lhsT=ckvn[:, k, kt * 128:(kt + 1) * 128],
                                               rhs=w_ukv_bf[:, k, 128 * h + 64:128 * h + 128], start=(k == 0), stop=(k == 1))
                        return ins
                    p.do('pe', mm, reads=[b_w] + all_ckvn, writes=[zb_])
                    p.do('dve', lambda e: e.tensor_copy(out=VT4[:, hs, t0_:t0_ + nt_, 0:64],
                                                        in_=z[:, 0:nt_ * 64].rearrange("p (t c) -> p t c", c=64)),
                         reads=[zb_], pwrites=[b_VT[hs]])
                tasks.append(tv)
            return tasks

        qcount = [0]

        def q_gen_tasks(h, qc, holder):
            def ta():
                qs = qcount[0] % 2
                qcount[0] += 1
                holder['qs'] = qs
                t0 = qc * 512
                p.dma('sp', 'csq%d' % qs, csq[64:96, qs, 0, :], cos_d[64:96, t0:t0 + 512], writes=[b_csq2[qs]])
                p.dma('sp', 'csq%d' % qs, csq[0:32, qs, 1, :], sin_d[0:32, t0:t0 + 512], pwrites=[b_csq2[qs]])
                za, zab = mnext()

                def mma(e):
                    ins = None
                    for k in range(3):
                        ins = e.matmul(za[0:96, :], lhsT=w_uq_bf[:, k, 96 * h:96 * h + 96], rhs=cqn[:, k, t0:t0 + 512], start=(k == 0), stop=(k == 2))
                    return ins
                p.do('pe', mma, reads=[b_w] + b_cqn, writes=[zab])
                b_QT[qs].begin()
                t1_, t1b = tq[2 * qs], b_tq[2 * qs]
                p.do('dve', lambda e: e.tensor_copy(out=QT[0:64, qs, :], in_=za[0:64, :]), reads=[zab], pwrites=[b_QT[qs]])
                p.do('dve', lambda e: e.tensor_tensor(out=t1_[64:96, :], in0=za[64:96, :], in1=csq[64:96, qs, 0, :], op=ALU.mult),
                     reads=[zab, b_csq2[qs]], writes=[t1b])

            def tb():
                qs = holder['qs']
                t0 = qc * 512
                t1_, t1b = tq[2 * qs], b_tq[2 * qs]
                t2_, t2b = tq[2 * qs + 1], b_tq[2 * qs + 1]
                zq, zqb = mnext()

                def mmb(e):
                    ins = None
                    for k in range(3):
                        ins = e.matmul(zq[0:32, :], lhsT=w_uqb_bf[:, k, 32 * h:32 * h + 32], rhs=cqn[:, k, t0:t0 + 512], start=(k == 0), stop=(k == 2))
                    return ins
                p.do('pe', mmb, reads=[b_w] + b_cqn, writes=[zqb])
                p.do('dve', lambda e: e.tensor_tensor(out=t2_[64:96, :], in0=zq[0:32, :], in1=csq[0:32, qs, 1, :], op=ALU.mult),
                     reads=[zqb, b_csq2[qs]], writes=[t2b])
                p.do('dve', lambda e: e.tensor_tensor(out=QT[64:96, qs, :], in0=t1_[64:96, :], in1=t2_[64:96, :], op=ALU.add),
                     reads=[t1b, t2b], pwrites=[b_QT[qs]])
            return [ta, tb]

        NB = (NKT + 1) // 2
        ocount = [0]
        scount = [0]
        pcount = [0]

        norm_prev = [None]

        def attention(h, qc, qs, pre_tasks, carry):
            carry_used = [bool(carry)]
            hs = h % 2
            os_ = 0
            osb = ocount[0] % 2
            ocount[0] += 1
            O = bank(6)
            pend = []

            def qk(b):
                ss = scount[0] % NSS
                scount[0] += 1
                S = bank(2 * ss, 2)
                nt_ = min(2, NKT - 2 * b)

                def mm(e):
                    ins = None
                    for j in range(nt_):
                        kt = 2 * b + j
                        ins = e.matmul(S[:, j * 512:(j + 1) * 512], lhsT=KT[0:96, hs, kt * 128:(kt + 1) * 128], rhs=QT[0:96, qs, :], start=True, stop=True)
                    return ins
                p.do('pe', mm, reads=[b_KT[hs], b_QT[qs]] + all_krope, writes=[Sb[ss]])
                ps_ = pcount[0] % NSS
                pcount[0] += 1
                p.do('act', lambda e: e.activation(out=PT[:, ps_, 0:nt_ * 512], in_=S[:, 0:nt_ * 512], func=AF.Exp, scale=SCALE),
                     reads=[Sb[ss]], writes=[b_PT[ps_]])
                pend.append((b, ps_, nt_))

            def pv_(b, ps_, nt_):
                def mm(e):
                    ins = None
                    for j in range(nt_):
                        kt = 2 * b + j
                        ins = e.matmul(O, lhsT=VT[:, hs, kt * 128:(kt + 1) * 128], rhs=PT[:, ps_, j * 512:(j + 1) * 512],
                                       start=(kt == 0), stop=(kt == NKT - 1))
                    return ins
                if b == 0:
                    p.do('pe', mm, reads=[b_VT[hs], b_PT[ps_]], writes=[Ob[os_]])
                else:
                    p.do('pe', mm, reads=[b_VT[hs], b_PT[ps_]], pwrites=[Ob[os_]])

            ptasks = list(pre_tasks)
            rs = osb

            def normalize():
                p.do('dve', lambda e: e.reciprocal(out=Rt[rs][0:64, :], in_=Osb[osb][64:128, :]), reads=[b_Osb[osb]], writes=[b_Rt[rs]])
                pb = (h % 2) * 64
                p.do('dve', lambda e: e.tensor_tensor(out=cat[pb:pb + 64, 4 + h // 2, 1 + qc * 512:1 + qc * 512 + 512], in0=Osb[osb][0:64, :], in1=Rt[rs][0:64, :], op=ALU.mult),
                     reads=[b_Osb[osb], b_Rt[rs]], writes=[b_cat[4 + h // 2][qc] if h % 2 == 0 else Buf()])

            def finish():
                if norm_prev[0] is not None:
                    norm_prev[0]()
                p.do('dve', lambda e: e.tensor_copy(out=Osb[osb], in_=O), reads=[Ob[os_]], writes=[b_Osb[osb]])
                norm_prev[0] = normalize

            for b in range(NB):
                qk(b)
                if b >= 2 and (b - 2) % 3 == 0 and ptasks:
                    ptasks.pop(0)()
                if b == 7 and norm_prev[0] is not None:
                    norm_prev[0]()
                    norm_prev[0] = None
                if carry:
                    carry.pop(0)()
                elif b >= 2 or not carry_used[0]:
                    if pend and (b >= 2):
                        pv_(*pend.pop(0))
            while ptasks:
                ptasks.pop(0)()
            while len(pend) > 2:
                pv_(*pend.pop(0))
            items = list(pend)
            pend.clear()
            left = [lambda it_=it_: pv_(*it_) for it_ in items[:-1]]
            last_ = items[-1]
            left.append(lambda: (pv_(*last_), finish()))
            return left

        b_scr = Buf()
        b_scr.begin()
        prep_steps = []
        if upto not in ('C1',):
            pst_f = A.alloc(4096, F32)
            pst_b = [A.alloc(2048, BF16) for _ in range(2)]
            b_pf, b_pb = Buf(), [Buf(), Buf()]
            wv = wups_d.rearrange("g p (k c) -> g p k c", k=8)
            pieces = []
            for k in range(8):
                for half in range(2):
                    for (c0_, nc_) in ((0, 1024), (1024, 1024), (2048, 768)):
                        g0_ = c0_ // 256
                        ng_ = nc_ // 256
                        pieces.append((wup_d[k * 128:(k + 1) * 128, half * 2816 + c0_:half * 2816 + c0_ + nc_], nc_,
                                       (lambda b_, k=k, half=half, g0_=g0_, ng_=ng_: (
                                           wv[g0_:g0_ + ng_, :, k, half * 256:(half + 1) * 256].rearrange("g p c -> p g c"),
                                           b_.rearrange("p (g c) -> p g c", c=256)))))
            for m in range(MFF):
                pieces.append((wdn_d[m * 128:(m + 1) * 128, :], 1024, (lambda b_, m=m: (wdns_d[:, m * 1024:(m + 1) * 1024], b_))))

            def stage_in(i):
                src_ap, ncols, dst_fn = pieces[i]
                p.dma('sp', 'ppf', pst_f[:, 0:ncols], src_ap, writes=[b_pf])
                p.do('dve', lambda e: e.tensor_copy(out=pst_b[i % 2][:, 0:ncols], in_=pst_f[:, 0:ncols]), reads=[b_pf], writes=[b_pb[i % 2]])

            def stage_out(i):
                src_ap, ncols, dst_fn = pieces[i]
                dst, srcv = dst_fn(pst_b[i % 2][:, 0:ncols])
                p.dma('sp', 'ppb%d' % (i % 2), dst, srcv, reads=[b_pb[i % 2]], pwrites=[b_scr])

            npc = len(pieces)
            nsteps = NH * NCH
            extra_ = [max(0, npc - nsteps)]
            stage_in(0)
            pi_ = [1]

            def prep_step():
                reps = 1
                if extra_[0] > 0:
                    reps = 2
                    extra_[0] -= 1
                for _ in range(reps):
                    i = pi_[0]
                    if i - 1 < npc and i - 1 >= 0 and i <= npc:
                        stage_out(i - 1)
                    if i < npc:
                        stage_in(i)
                    pi_[0] += 1
            prep_steps.append(prep_step)
        mring['banks'] = [0, 1, 2, 3, 4, 5, 6]
        for tsk in kv_tasks(0):
            tsk()
        mring['banks'] = None
        for bk in range(6):
            _merge(Sb[bk // 2].rd, Mfree[bk].rd)
            _merge(Sb[bk // 2].wr, Mfree[bk].wr)
        _merge(Ob[0].rd, Mfree[6].rd)
        _merge(Ob[0].wr, Mfree[6].wr)
        order = [(h, qc) for h in range(NH) for qc in range(NCH)]
        holder = {}
        carry_ = []
        for tsk in q_gen_tasks(0, 0, holder):
            tsk()
        for idx, (h, qc) in enumerate(order):
            pre = []
            nholder = {}
            if idx + 1 < len(order):
                hn, qn = order[idx + 1]
                pre.extend(q_gen_tasks(hn, qn, nholder))
            if h + 1 < NH:
                tks = kv_tasks(h + 1)
                per = (len(tks) + NCH - 1) // NCH
                pre.extend(tks[qc * per:(qc + 1) * per])
            if prep_steps:
                prep_steps[0]()
            carry_ = attention(h, qc, holder['qs'], pre, carry_)
            holder = nholder
        for f_ in carry_:
            f_()
        if norm_prev[0] is not None:
            norm_prev[0]()
            norm_prev[0] = None
        if prep_steps:
            while pi_[0] <= npc:
                prep_steps[0]()
        p.barrier()
        A.release()
        if upto == 'C1':
            dump_cat([('d_attn0', 4, 128), ('d_attn3', 7, 128)])
            finish()
            return nc

        if off_dead1 - off_dead0 >= 59000:
            A2 = Arena(arena_t[:, off_dead0:off_dead1], off_dead1 - off_dead0)
        else:
            A2 = Arena(A.raw(59392), 59392)
        A.mark()
        A2.mark()
        w_o_bf = A.alloc(8 * 1024 * 2, BF16).rearrange("p (k c) -> p k c", k=8)
        g1bc = A.alloc(4096, F32)
        statsA = A.alloc(40 * 2 * 4, F32).rearrange("p (t s) -> p t s", s=2)
        rT0 = A.alloc(8 * 2048, F32).rearrange("p (j t) -> p j t", j=8)
        rbb = [A.alloc(1024, BF16) for _ in range(3)]
        rsq = [A.alloc(1024, BF16) for _ in range(3)]
        b_rbb = [Buf() for _ in range(3)]
        b_rsq = [Buf() for _ in range(3)]
        meanc = [A.alloc(2048, F32) for _ in range(2)]
        m2c = [A.alloc(2048, F32) for _ in range(2)]
        rstc = [A.alloc(2048, F32) for _ in range(2)]
        b_meanc, b_m2c, b_rstc = [Buf(), Buf()], [Buf(), Buf()], [Buf(), Buf()]
        xst = [A.alloc(2048, F32) for _ in range(3)]
        b_xst = [Buf() for _ in range(3)]
        wst2 = [A.alloc(4096, F32) for _ in range(2)]
        b_wst2 = [Buf(), Buf()]
        xr = [A2.alloc(4096, F32) for _ in range(8)]
        b_xr = [Buf() for _ in range(8)]
        rT1 = A2.alloc(8 * 2048, F32).rearrange("p (j t) -> p j t", j=8)
        rTs = [rT0, rT1]
        b_rTs = [[Buf() for _ in range(8)] for _ in range(2)]
        b_wo, b_g1bc, b_statsA = Buf(), Buf(), Buf()
        p.do('dve', lambda e: e.tensor_scalar(out=statsA.rearrange("p t s -> p (t s)"), in0=stats.rearrange("p t s -> p (t s)"),
                                              scalar1=ALPHA, scalar2=None, op0=ALU.mult), reads=[b_stats], writes=[b_statsA])
        dg = A2.alloc(8 * 512, F32).rearrange("p (j c) -> p j c", j=8)
        b_dg = Buf()
        b_dg.begin()
        for j in range(8):
            p.do('dve', lambda e, j=j: e.tensor_scalar(out=dg[:, j, :], in0=identF, scalar1=modL(2, j), scalar2=None, op0=ALU.mult),
                 reads=[b_dcol, b_const], pwrites=[b_dg])
        gb_b = [Buf(), Buf()]
        for hb in range(2):
            def mm(e, hb=hb):
                ins = None
                for jj in range(4):
                    j = hb * 4 + jj
                    ins = e.matmul(bank(hb)[:, jj * 128:(jj + 1) * 128], lhsT=onesF, rhs=dg[:, j, :], start=True, stop=True)
                return ins
            p.do('pe', mm, reads=[b_dg], writes=[gb_b[hb]], extra=tconst)
            if hb == 0:
                p.do('act', lambda e, hb=hb: e.activation(out=g1bc[:, hb * 512:(hb + 1) * 512], in_=bank(hb), func=AF.Copy), reads=[gb_b[hb]], writes=[b_g1bc])
            else:
                p.do('act', lambda e, hb=hb: e.activation(out=g1bc[:, hb * 512:(hb + 1) * 512], in_=bank(hb), func=AF.Copy), reads=[gb_b[hb]], pwrites=[b_g1bc])
        b_wo.begin()
        for k in range(8):
            st_ = wst2[k % 2]
            p.dma('sp', 'wst%d' % (k % 2), st_, wo_d[k * 128:(k + 1) * 128, :], writes=[b_wst2[k % 2]])
            p.do('dve', lambda e, k=k, st_=st_: e.tensor_tensor(out=w_o_bf[:, k, :], in0=st_, in1=g1bc, op=ALU.mult),
                 reads=[b_wst2[k % 2], b_g1bc], pwrites=[b_wo])
        ybk = [Buf() for _ in range(4)]
        stb = [Buf() for _ in range(4)]
        xrc = [0]
        ri = [0]
        xsi = [0]
        xs_of = {}

        def c2a_xload(ci):
            t0 = ci * 512
            xs_ = []
            for i in range(4):
                xi = xrc[0] % 8
                xrc[0] += 1
                p.dma('sp', 'xr%d' % xi, xr[xi], x_d[t0 + i * 128:t0 + (i + 1) * 128, :], writes=[b_xr[xi]])
                xs_.append(xi)
            xs_of[ci] = xs_

        def c2a_xscale(ci):
            for i in range(4):
                xi = xs_of[ci][i]
                gt = ci * 4 + i
                p.do('act', lambda e, xi=xi, gt=gt: e.activation(out=xr[xi], in_=xr[xi], func=AF.Identity, scale=statsA[:, gt, 0:1], bias=statsA[:, gt, 1:2]),
                     reads=[b_xr[xi], b_statsA], writes=[b_xr[xi]])

        def c2a_front(ci):
            t0 = ci * 512
            sl = ci % 2
            rT = rTs[sl]
            b_rT = b_rTs[sl]
            if ci + 1 < NCH:
                c2a_xload(ci + 1)
            xs_ = xs_of[ci]
            if ci >= 1:
                c2a_back_stats(ci - 1)
            zsum, zsq = bank(4 + sl), bank(6 + sl)
            pending = []

            def stat_mm(j, rb_i, first, last):
                p.do('pe', lambda e: e.matmul(zsum, lhsT=onesB, rhs=rbb[rb_i], start=first, stop=last), reads=[b_rbb[rb_i]],
                     writes=[stb[sl]] if first else (), pwrites=() if first else [stb[sl]], extra=tconst)
                p.do('pe', lambda e: e.matmul(zsq, lhsT=onesB, rhs=rsq[rb_i], start=first, stop=last), reads=[b_rsq[rb_i]],
                     writes=[stb[2 + sl]] if first else (), pwrites=() if first else [stb[2 + sl]], extra=tconst)

            for j in range(8):
                z = bank(j % 4)

                def mm(e, z=z, j=j):
                    ins = None
                    for k in range(8):
                        ins = e.matmul(z, lhsT=w_o_bf[:, k, j * 128:(j + 1) * 128], rhs=cat[:, k, 1 + t0:1 + t0 + 512], start=(k == 0), stop=False)
                    for i in range(4):
                        ins = e.matmul(z[:, i * 128:(i + 1) * 128], lhsT=xr[xs_[i]][:, j * 128:(j + 1) * 128], rhs=identF, start=False, stop=(i == 3))
                    return ins
                p.do('pe', mm, reads=[b_wo] + [b_cat[k][ci] for k in range(8)] + [b_xr[x] for x in xs_], writes=[ybk[j % 4]], extra=tconst)
                p.do('act', lambda e, z=z, j=j: e.activation(out=rT[:, j, :], in_=z, func=AF.Identity, bias=dcv('g1bo', j), scale=1.0),
                     reads=[ybk[j % 4], b_dcol], writes=[b_rT[j]])
                r_i = ri[0] % 3
                ri[0] += 1
                p.do('act', lambda e, j=j, r_i=r_i: e.activation(out=rbb[r_i], in_=rT[:, j, :], func=AF.Copy), reads=[b_rT[j]], writes=[b_rbb[r_i]])
                p.do('dve', lambda e, j=j, r_i=r_i: e.tensor_tensor(out=rsq[r_i], in0=rT[:, j, :], in1=rT[:, j, :], op=ALU.mult), reads=[b_rT[j]], writes=[b_rsq[r_i]])
                pending.append((j, r_i))
                if len(pending) > 2:
                    jj, rr = pending.pop(0)
                    stat_mm(jj, rr, jj == 0, False)
                if ci >= 1:
                    c2a_back_j(ci - 1, j)
            while pending:
                jj, rr = pending.pop(0)
                stat_mm(jj, rr, jj == 0, jj == 7)
            if ci + 1 < NCH:
                c2a_xscale(ci + 1)

        def c2a_back_stats(ci):
            sl = ci % 2
            zsum, zsq = bank(4 + sl), bank(6 + sl)
            p.do('act', lambda e, sl=sl: e.activation(out=meanc[sl], in_=zsum, func=AF.Copy, scale=1.0 / 1024), reads=[stb[sl]], writes=[b_meanc[sl]])
            p.do('dve', lambda e, sl=sl: e.tensor_tensor(out=m2c[sl], in0=meanc[sl], in1=meanc[sl], op=ALU.mult), reads=[b_meanc[sl]], writes=[b_m2c[sl]])
            p.do('dve', lambda e, sl=sl: e.scalar_tensor_tensor(out=rstc[sl], in0=zsq, scalar=1.0 / 1024, in1=m2c[sl], op0=ALU.mult, op1=ALU.subtract),
                 reads=[stb[2 + sl], b_m2c[sl]], writes=[b_rstc[sl]])
            p.do('act', lambda e, sl=sl: e.activation(out=rstc[sl], in_=rstc[sl], func=AF.Sqrt, bias=eps_c, scale=1.0), reads=[b_rstc[sl]], writes=[b_rstc[sl]], extra=tconst)
            p.do('dve', lambda e, sl=sl: e.reciprocal(out=rstc[sl], in_=rstc[sl]), reads=[b_rstc[sl]], writes=[b_rstc[sl]])

        def c2a_back_j(ci, j):
            t0 = ci * 512
            sl = ci % 2
            rT = rTs[sl]
            b_rT = b_rTs[sl]
            p.do('dve', lambda e: e.tensor_tensor(out=rT[:, j, :], in0=rT[:, j, :], in1=meanc[sl], op=ALU.subtract),
                 reads=[b_rT[j], b_meanc[sl]], writes=[b_rT[j]])
            p.do('dve', lambda e: e.tensor_tensor(out=rT[:, j, :], in0=rT[:, j, :], in1=rstc[sl], op=ALU.mult),
                 reads=[b_rT[j], b_rstc[sl]], writes=[b_rT[j]])
            p.do('act', lambda e: e.activation(out=cat[:, j, 1 + t0:1 + t0 + 512], in_=rT[:, j, :], func=AF.Identity, scale=dcv('G2', j), bias=dcv('B2', j)),
                 reads=[b_rT[j], b_dcol], writes=[b_cat[j][ci]])
            x_i = xsi[0] % 3
            xsi[0] += 1
            p.do('dve', lambda e: e.tensor_scalar(out=xst[x_i], in0=rT[:, j, :], scalar1=dcv('A3', j), scalar2=dcv('B3', j), op0=ALU.mult, op1=ALU.add),
                 reads=[b_rT[j], b_dcol], writes=[b_xst[x_i]])
            p.dma('sp', 'xst%d' % x_i, xts_d[j, :, t0:t0 + 512], xst[x_i], reads=[b_xst[x_i]])

        c2a_xload(0)
        c2a_xscale(0)
        for ci in range(NCH):
            c2a_front(ci)
        c2a_back_stats(NCH - 1)
        for j in range(8):
            c2a_back_j(NCH - 1, j)
        p.barrier()
        A.release()
        A2.release()
        if upto == 'C2a':
            dump_cat([('d_h2_0', 0, 128), ('d_h2_7', 7, 128)])
            p.barrier()
            A.mark()
            dt2 = A.alloc(N * 4, F32)
            tk = p.dma('sp', 'dbg4', dt2, xts_d[3, :, :])
            p.dma('sp', 'dbg4', dbg_d['d_xt3'], dt2, extra=[tk])
            finish()
            return nc

        A.mark()
        A2.mark()
        wdn_bf = A.alloc(MFF * 1024 * 2, BF16).rearrange("p (m c) -> p m c", m=MFF)
        b_wdn = Buf()
        NWUG = 3
        wug = [A.alloc(8 * 512 * 2, BF16).rearrange("p (k c) -> p k c", k=8) for _ in range(NWUG)]
        b_wug = [Buf() for _ in range(NWUG)]
        aT = A2.alloc(MFF * 512 * 2, BF16).rearrange("p (m t) -> p m t", m=MFF)
        b_aT = [Buf() for _ in range(MFF)]
        cg = [A2.alloc(2048, F32) for _ in range(2)]
        cv = [A2.alloc(2048, F32) for _ in range(2)]
        sgb = [A2.alloc(2048, F32) for _ in range(2)]
        b_cg, b_cv, b_sg = [Buf(), Buf()], [Buf(), Buf()], [Buf(), Buf()]
        r2T = A2.alloc(8 * 2048, F32).rearrange("p (j t) -> p j t", j=8)
        b_r2 = [Buf() for _ in range(8)]
        ost = [A2.alloc(4096, F32) for _ in range(2)]
        b_ost = [Buf(), Buf()]
        rbb2 = [A.alloc(1024, BF16) for _ in range(2)]
        rsq2 = [A.alloc(1024, BF16) for _ in range(2)]
        b_rbb2 = [Buf() for _ in range(2)]
        b_rsq2 = [Buf() for _ in range(2)]
        mean2 = [A.alloc(2048, F32) for _ in range(1)] * 2
        m22 = [A.alloc(2048, F32) for _ in range(1)] * 2
        rst2 = [A.alloc(2048, F32) for _ in range(1)] * 2
        b_mean2, b_m22, b_rst2 = [Buf()] * 2, [Buf()] * 2, [Buf()] * 2
        b_h2 = Buf()
        p.do('pool', lambda e: e.memset(cat[:, :, 0:1], 0.0), writes=[b_h2])
        p.do('pool', lambda e: e.memset(cat[:, :, N + 1:N + 2], 0.0), pwrites=[b_h2])
        nchk = (N + 455) // 456
        base = N // nchk
        rem = N - base * nchk
        ranges = []
        s0 = 0
        for c_ in range(nchk):
            w_ = base + (1 if c_ < rem else 0)
            ranges.append((s0, s0 + w_))
            s0 += w_
        ubk = [Buf() for _ in range(4)]
        y2bk = [Buf(), Buf()]
        st2b = [Buf(), Buf()]
        gi = [0]
        ui = [0]
        ci2 = [0]
        xli = [0]
        r2i = [0]
        oi = [0]
        FW = PV['ffn_w'][0]

        def fw(m, tap):
            return pvec[:, FW + m * 3 + tap:FW + m * 3 + tap + 1]

        def xt_prefetch(rc):
            s_, e_ = ranges[rc]
            w_ = e_ - s_
            for j in range(8):
                p.dma('sp', 'xld%d' % j, r2T[:, j, 0:w_], xts_d[j, :, s_:e_], writes=[b_r2[j]])

        wq = {'issued': 0}
        NGRP = MFF // 2
        TOTG = NGRP * nchk

        def wug_issue(upto_g):
            while wq['issued'] < min(upto_g, TOTG):
                gq = wq['issued']
                g_ = gq % NWUG
                p.dma('sp', 'wug%d' % g_, wug[g_].rearrange("p k c -> p (k c)"), wups_d[gq % NGRP], reads=[b_scr], writes=[b_wug[g_]])
                wq['issued'] += 1

        def up_pairs(rc, m_list, state):
            s_, e_ = ranges[rc]
            w_ = e_ - s_
            for m in m_list:
                if m % 2 == 0:
                    gq = rc * NGRP + m // 2
                    wug_issue(gq + NWUG)
                    state['g'] = gq % NWUG
                g_ = state['g']
                zs = []
                for part in range(2):
                    u_ = ui[0] % 4
                    ui[0] += 1
                    z = bank(u_)
                    c0 = part * 256 + (m % 2) * 128

                    def mm(e, z=z, c0=c0, g_=g_):
                        ins = None
                        for k in range(8):
                            ins = e.matmul(z[:, 0:w_ + 2], lhsT=wug[g_][:, k, c0:c0 + 128], rhs=cat[:, k, s_:e_ + 2], start=(k == 0), stop=(k == 7))
                        return ins
                    p.do('pe', mm, reads=[b_wug[g_], b_h2] + [b_cat[k][cc] for k in range(8) for cc in range(NCH)], writes=[ubk[u_]])
                    zs.append((z, ubk[u_]))
                c_i = ci2[0] % 2
                ci2[0] += 1
                for part, (dst, bdst) in enumerate([(cg[c_i], b_cg[c_i]), (cv[c_i], b_cv[c_i])]):
                    z, zb_ = zs[part]
                    mm_ = m + part * MFF
                    p.do('act', lambda e, z=z, dst=dst, mm_=mm_: e.activation(out=dst[:, 0:w_], in_=z[:, 1:w_ + 1], func=AF.Identity, scale=fw(mm_, 1),
                                                                          bias=pv('ffn_b', mm_)), reads=[zb_, b_pvec], writes=[bdst])
                    p.do('dve', lambda e, z=z, dst=dst, mm_=mm_: e.scalar_tensor_tensor(out=dst[:, 0:w_], in0=z[:, 0:w_], scalar=fw(mm_, 0), in1=dst[:, 0:w_],
                                                                                op0=ALU.mult, op1=ALU.add), reads=[zb_, bdst, b_pvec], writes=[bdst])
                    p.do('dve', lambda e, z=z, dst=dst, mm_=mm_: e.scalar_tensor_tensor(out=dst[:, 0:w_], in0=z[:, 2:w_ + 2], scalar=fw(mm_, 2), in1=dst[:, 0:w_],
                                                                                op0=ALU.mult, op1=ALU.add), reads=[zb_, bdst, b_pvec], writes=[bdst])
                p.do('act', lambda e, c_i=c_i: e.activation(out=sgb[c_i][:, 0:w_], in_=cg[c_i][:, 0:w_], func=AF.Silu), reads=[b_cg[c_i]], writes=[b_sg[c_i]])
                p.do('pool', lambda e, c_i=c_i, m=m: e.tensor_tensor(out=aT[:, m, 0:w_], in0=sgb[c_i][:, 0:w_], in1=cv[c_i][:, 0:w_], op=ALU.mult),
                     reads=[b_sg[c_i], b_cv[c_i]], writes=[b_aT[m]])

        def down_and_ln(rc, mid=None):
            s_, e_ = ranges[rc]
            w_ = e_ - s_
            zsum, zsq = bank(6), bank(7)
            pending = []

            def stat_mm(jj, rr, first, last):
                p.do('pe', lambda e: e.matmul(zsum[:, 0:w_], lhsT=onesB, rhs=rbb2[rr][:, 0:w_], start=first, stop=last), reads=[b_rbb2[rr]],
                     writes=[st2b[0]] if first else (), pwrites=() if first else [st2b[0]], extra=tconst)
                p.do('pe', lambda e: e.matmul(zsq[:, 0:w_], lhsT=onesB, rhs=rsq2[rr][:, 0:w_], start=first, stop=last), reads=[b_rsq2[rr]],
                     writes=[st2b[1]] if first else (), pwrites=() if first else [st2b[1]], extra=tconst)

            for j in range(8):
                yb = j % 2
                z = bank(4 + yb)

                def mm(e, z=z, j=j):
                    ins = None
                    for m in range(MFF):
                        ins = e.matmul(z[:, 0:w_], lhsT=wdn_bf[:, m, j * 128:(j + 1) * 128], rhs=aT[:, m, 0:w_], start=(m == 0), stop=(m == MFF - 1))
                    return ins
                p.do('pe', mm, reads=[b_wdn] + b_aT, writes=[y2bk[yb]])
                p.do('dve', lambda e, z=z, j=j: e.scalar_tensor_tensor(out=r2T[:, j, 0:w_], in0=z[:, 0:w_], scalar=modL(5, j), in1=r2T[:, j, 0:w_],
                                                                     op0=ALU.mult, op1=ALU.add), reads=[y2bk[yb], b_r2[j], b_dcol], writes=[b_r2[j]])
                r_i = r2i[0] % 2
                r2i[0] += 1
                p.do('act', lambda e, j=j, r_i=r_i: e.activation(out=rbb2[r_i][:, 0:w_], in_=r2T[:, j, 0:w_], func=AF.Copy), reads=[b_r2[j]], writes=[b_rbb2[r_i]])
                p.do('act', lambda e, j=j, r_i=r_i: e.activation(out=rsq2[r_i][:, 0:w_], in_=r2T[:, j, 0:w_], func=AF.Square), reads=[b_r2[j]], writes=[b_rsq2[r_i]])
                pending.append((j, r_i))
                if len(pending) > 1:
                    jj, rr = pending.pop(0)
                    stat_mm(jj, rr, jj == 0, False)
            while pending:
                jj, rr = pending.pop(0)
                stat_mm(jj, rr, jj == 0, jj == 7)
            sl = rc % 2
            p.do('act', lambda e: e.activation(out=mean2[sl][:, 0:w_], in_=zsum[:, 0:w_], func=AF.Copy, scale=1.0 / 1024), reads=[st2b[0]], writes=[b_mean2[sl]])
            p.do('dve', lambda e: e.tensor_tensor(out=m22[sl][:, 0:w_], in0=mean2[sl][:, 0:w_], in1=mean2[sl][:, 0:w_], op=ALU.mult), reads=[b_mean2[sl]], writes=[b_m22[sl]])
            p.do('dve', lambda e: e.scalar_tensor_tensor(out=rst2[sl][:, 0:w_], in0=zsq[:, 0:w_], scalar=1.0 / 1024, in1=m22[sl][:, 0:w_], op0=ALU.mult, op1=ALU.subtract),
                 reads=[st2b[1], b_m22[sl]], writes=[b_rst2[sl]])
            p.do('act', lambda e: e.activation(out=rst2[sl][:, 0:w_], in_=rst2[sl][:, 0:w_], func=AF.Sqrt, bias=eps_c, scale=1.0), reads=[b_rst2[sl]], writes=[b_rst2[sl]], extra=tconst)
            p.do('dve', lambda e: e.reciprocal(out=rst2[sl][:, 0:w_], in_=rst2[sl][:, 0:w_]), reads=[b_rst2[sl]], writes=[b_rst2[sl]])
            if mid is not None:
                mid(0)
            for j in range(8):
                if j == 4 and mid is not None:
                    mid(1)
                p.do('dve', lambda e, j=j: e.tensor_tensor(out=r2T[:, j, 0:w_], in0=r2T[:, j, 0:w_], in1=mean2[sl][:, 0:w_], op=ALU.subtract),
                     reads=[b_r2[j], b_mean2[sl]], writes=[b_r2[j]])
                p.do('dve', lambda e, j=j: e.tensor_tensor(out=r2T[:, j, 0:w_], in0=r2T[:, j, 0:w_], in1=rst2[sl][:, 0:w_], op=ALU.mult),
                     reads=[b_r2[j], b_rst2[sl]], writes=[b_r2[j]])
                p.do('act', lambda e, j=j: e.activation(out=r2T[:, j, 0:w_], in_=r2T[:, j, 0:w_], func=AF.Identity, scale=pv('ln2_g', j), bias=pv('ln2_b', j)),
                     reads=[b_r2[j], b_pvec], writes=[b_r2[j]])
            if mid is not None:
                mid(2)

        def out_transposes(rc):
            s_, e_ = ranges[rc]
            w_ = e_ - s_
            for it_, i0 in enumerate(range(0, w_, 128)):
                wi = min(128, w_ - i0)
                T = bank(4, 2) if it_ % 2 == 0 else bank(6, 2)
                tb0, tb1 = (y2bk[0], y2bk[1]) if it_ % 2 == 0 else (st2b[0], st2b[1])

                def mm(e, i0=i0, wi=wi, T=T):
                    ins = None
                    for j in range(8):
                        ins = e.transpose(out=T[0:wi, j * 128:(j + 1) * 128], in_=r2T[:, j, i0:i0 + wi], identity=identF)
                    return ins
                tb0.begin()
                tb1.begin()
                w = _merge(_merge({}, tb0.prev), tb1.prev)
                tok = p.do('pe', mm, reads=b_r2, extra=[w, tconst])
                _merge(tb0.wr, tok)
                _merge(tb1.wr, tok)
                o_ = oi[0] % 2
                oi[0] += 1
                tk = p.do('act', lambda e, o_=o_, wi=wi, T=T: e.activation(out=ost[o_][0:wi, :], in_=T[0:wi, :], func=AF.Copy),
                          reads=[tb0, tb1], writes=[b_ost[o_]])
                p.dma('sp', 'ost%d' % o_, out_d[s_ + i0:s_ + i0 + wi, :], ost[o_][0:wi, :], reads=[b_ost[o_]])

        NPRE = 6
        state = {}
        xt_prefetch(0)
        up_pairs(0, list(range(4)), state)
        p.dma('sp', 'wdnld', wdn_bf.rearrange("p m c -> p (m c)"), wdns_d, reads=[b_scr], writes=[b_wdn])
        up_pairs(0, list(range(4, MFF)), state)
        for rc in range(nchk):
            if rc + 1 < nchk:
                down_and_ln(rc, mid=lambda part, rc=rc: up_pairs(rc + 1, [[0, 1], [2, 3], list(range(4, NPRE))][part], state))
            else:
                down_and_ln(rc)
            out_transposes(rc)
            if rc + 1 < nchk:
                up_pairs(rc + 1, list(range(NPRE, 14)), state)
                xt_prefetch(rc + 1)
                up_pairs(rc + 1, list(range(14, MFF)), state)
        finish()
    return nc


def kernel(**inputs):
    inp = {k: np.asarray(v) for k, v in inputs.items()}
    B, N, _ = inp['x'].shape
    NC = inp['ctx'].shape[1]
    nc = build_program(N=N, NC=NC)
    cosT, sinT = rope_tables(N)
    ident = np.eye(128, dtype=np.float32)
    shared = {
        'ident': ident, 'cosT': cosT, 'sinT': sinT,
        'w_ada': np.ascontiguousarray(inp['w_ada'][0], dtype=np.float32),
        'w_in': np.ascontiguousarray(inp['w_in'][0], dtype=np.float32),
        'w_uq': np.ascontiguousarray(inp['w_uq'][0], dtype=np.float32),
        'w_ukv': np.ascontiguousarray(inp['w_ukv'][0], dtype=np.float32),
        'w_o': np.ascontiguousarray(inp['w_o'][0], dtype=np.float32),
        'w_up': np.ascontiguousarray(inp['w_up'][0], dtype=np.float32),
        'w_down': np.ascontiguousarray(inp['w_down'][0], dtype=np.float32),
    }
    in_maps = []
    for b in range(B):
        m = dict(shared)
        m['x'] = np.ascontiguousarray(inp['x'][b], dtype=np.float32)
        m['ctx'] = np.ascontiguousarray(inp['ctx'][b], dtype=np.float32)
        m['pvec'] = pack_pvec(inp, b)
        in_maps.append(m)
    res = run_bass_kernel_spmd(nc, in_maps, core_ids=list(range(B)))
    return np.stack([np.asarray(r['out'], dtype=np.float32) for r in res.results], axis=0)
```
